# Optimizing a Trainium2 kernel written in Bass

```python
import jax, jax.numpy as jnp
from jax import lax
import numpy as np

D_MODEL = 1024
BATCH = 8
SEQ = 2048
DEPTH = 2
DEC_BATCH = 128
DEC_SEQ = 1
PAST_LEN = 16384
PAGE_SIZE = 128

N_HEADS = 8
HEAD_DIM_K = 128
HEAD_DIM_V = 128
QK_DIM = N_HEADS * HEAD_DIM_K
V_DIM = N_HEADS * HEAD_DIM_V
QKV_DIM = 2 * QK_DIM + V_DIM
SHORT_CONV = 4
CHUNK = 64
CONF_DIM = D_MODEL // 2
CONF_CONV = 31
D_FF = 2816
FFN_CONV = 3
IN_COLS = QKV_DIM + V_DIM + 2 * N_HEADS + 2 * CONF_DIM + 2 * D_MODEL
NORM_EPS = 1e-6

kernel_name = "hybrid_gated_deltanet_conformer_convffn_step"


def rmsnorm(x, w):
    x32 = x.astype(jnp.float32)
    y = x32 * lax.rsqrt(jnp.mean(x32 * x32, axis=-1, keepdims=True) + NORM_EPS)
    return (y * w.astype(jnp.float32)).astype(x.dtype)


def layernorm(x, w, b):
    x32 = x.astype(jnp.float32)
    mu = jnp.mean(x32, axis=-1, keepdims=True)
    xc = x32 - mu
    y = xc * lax.rsqrt(jnp.mean(xc * xc, axis=-1, keepdims=True) + NORM_EPS)
    return (y * w.astype(jnp.float32) + b.astype(jnp.float32)).astype(x.dtype)


def l2norm(x):
    x32 = x.astype(jnp.float32)
    return x32 * lax.rsqrt(jnp.sum(x32 * x32, axis=-1, keepdims=True) + NORM_EPS)


def split_cols(a, sizes):
    out, start = [], 0
    for s in sizes:
        out.append(a[..., start:start + s])
        start += s
    return out


def causal_dwconv(x, buf, w):
    n_ch = x.shape[-1]
    width = w.shape[0]
    xp = jnp.concatenate([buf.astype(x.dtype), x], axis=1)
    y = lax.conv_general_dilated(xp, w.astype(x.dtype)[:, None, :], window_strides=(1,), padding='VALID',
                                 dimension_numbers=('NWC', 'WIO', 'NWC'), feature_group_count=n_ch)
    return y, xp[:, xp.shape[1] - (width - 1):]


def to_chunks(a, n):
    b, h = a.shape[0], a.shape[2]
    a = a.reshape((b, n, CHUNK, h) + a.shape[3:])
    return jnp.moveaxis(a, 3, 1)


def gated_delta_rule(q, k, v, beta, g, s0):
    b, t, h = q.shape[0], q.shape[1], q.shape[2]
    n = -(-t // CHUNK)
    pad = n * CHUNK - t

    def prep(a):
        a = a.astype(jnp.float32)
        a = jnp.pad(a, [(0, 0), (0, pad)] + [(0, 0)] * (a.ndim - 2))
        return to_chunks(a, n)

    q = prep(q) * (HEAD_DIM_K ** -0.5)
    k, v, beta, g = prep(k), prep(v), prep(beta), prep(g)
    gc = jnp.cumsum(g, axis=-1)
    idx = jnp.arange(CHUNK)
    causal = idx[:, None] >= idx[None, :]
    strict = idx[:, None] > idx[None, :]
    decay = jnp.exp(jnp.where(causal, gc[..., :, None] - gc[..., None, :], -jnp.inf))
    kb = k * beta[..., None]
    vb = v * beta[..., None]
    low = jnp.where(strict, jnp.einsum('bhnik,bhnjk->bhnij', kb, k) * decay, 0.0)
    a_mat = low + jnp.eye(CHUNK, dtype=jnp.float32)
    rhs = jnp.concatenate([vb, kb * jnp.exp(gc)[..., None]], axis=-1)
    sol = lax.linalg.triangular_solve(a_mat, rhs, left_side=True, lower=True, unit_diagonal=True)
    u, w = sol[..., :HEAD_DIM_V], sol[..., HEAD_DIM_V:]
    qk = jnp.einsum('bhnik,bhnjk->bhnij', q, k) * decay
    qg = q * jnp.exp(gc)[..., None]
    kg = k * jnp.exp(gc[..., -1:] - gc)[..., None]
    glast = jnp.exp(gc[..., -1])
    xs = tuple(jnp.moveaxis(a, 2, 0) for a in (qg, kg, u, w, qk, glast))

    def step(s, xc):
        qg_c, kg_c, u_c, w_c, qk_c, gl_c = xc
        v_new = u_c - jnp.einsum('bhck,bhkv->bhcv', w_c, s)
        o = jnp.einsum('bhck,bhkv->bhcv', qg_c, s) + jnp.einsum('bhij,bhjv->bhiv', qk_c, v_new)
        s = s * gl_c[..., None, None] + jnp.einsum('bhck,bhcv->bhkv', kg_c, v_new)
        return s, o

    s_final, o = lax.scan(step, s0.astype(jnp.float32), xs)
    o = jnp.transpose(o, (1, 0, 3, 2, 4)).reshape(b, n * CHUNK, h, HEAD_DIM_V)[:, :t]
    return o, s_final


def token_mixer(h, s_delta, s_qkv, s_conf, w_in, conv_qkv_w, a_log, dt_bias, delta_norm_w, w_o_delta,
                conf_conv_w, conf_conv_b, conf_ln_w, conf_ln_b, w_o_conf, w_out):
    b, t = h.shape[0], h.shape[1]
    proj = h @ w_in
    qkv, z, beta_in, alpha_in, glu, gate_a, gate_b = split_cols(
        proj, [QKV_DIM, V_DIM, N_HEADS, N_HEADS, 2 * CONF_DIM, D_MODEL, D_MODEL])
    qkv_c, new_qkv = causal_dwconv(qkv, s_qkv, conv_qkv_w)
    qkv_c = jax.nn.silu(qkv_c)
    q, k, v = split_cols(qkv_c, [QK_DIM, QK_DIM, V_DIM])
    q = l2norm(q.reshape(b, t, N_HEADS, HEAD_DIM_K))
    k = l2norm(k.reshape(b, t, N_HEADS, HEAD_DIM_K))
    v = v.reshape(b, t, N_HEADS, HEAD_DIM_V)
    beta = jax.nn.sigmoid(beta_in.astype(jnp.float32))
    g = -jnp.exp(a_log.astype(jnp.float32)) * jax.nn.softplus(alpha_in.astype(jnp.float32) + dt_bias.astype(jnp.float32))
    o, new_delta = gated_delta_rule(q, k, v, beta, g, s_delta)
    o = o * lax.rsqrt(jnp.mean(o * o, axis=-1, keepdims=True) + NORM_EPS) * delta_norm_w.astype(jnp.float32)
    o = o * jax.nn.silu(z.astype(jnp.float32).reshape(b, t, N_HEADS, HEAD_DIM_V))
    y_a = o.reshape(b, t, V_DIM).astype(h.dtype) @ w_o_delta
    u = glu[..., :CONF_DIM] * jax.nn.sigmoid(glu[..., CONF_DIM:])
    c, new_conf = causal_dwconv(u, s_conf, conf_conv_w)
    c = jax.nn.silu(layernorm(c + conf_conv_b, conf_ln_w, conf_ln_b))
    y_b = c @ w_o_conf
    merged = jax.nn.sigmoid(gate_a) * y_a + jax.nn.sigmoid(gate_b) * y_b
    return merged @ w_out, new_delta.astype(s_delta.dtype), new_qkv, new_conf


def conv_ffn(h, s_ffn, w_up, ffn_conv_w, ffn_conv_b, w_down):
    up, new_ffn = causal_dwconv(h @ w_up, s_ffn, ffn_conv_w)
    up = up + ffn_conv_b
    return (jax.nn.silu(up[..., :D_FF]) * up[..., D_FF:]) @ w_down, new_ffn


def setup_inputs(seed: int = 0) -> dict:
    key = jax.random.key(seed)
    ks = jax.random.split(key, 32)
    f32 = jnp.float32

    def nrm(k, shape, scale):
        return jax.random.normal(k, shape, f32) * scale

    def gain(k, shape):
        return 1.0 + 0.01 * jax.random.normal(k, shape, f32)

    dt = jnp.exp(jax.random.uniform(ks[10], (DEPTH, N_HEADS), f32, np.log(1e-3), np.log(1e-1)))
    return {
        'x_prompt': nrm(ks[0], (BATCH, SEQ, D_MODEL), 1.0),
        'x_sample': nrm(ks[1], (DEC_BATCH, DEC_SEQ, D_MODEL), 1.0),
        'state_delta': nrm(ks[2], (DEPTH, DEC_BATCH, N_HEADS, HEAD_DIM_K, HEAD_DIM_V), 0.1),
        'state_qkv_conv': nrm(ks[3], (DEPTH, DEC_BATCH, SHORT_CONV - 1, QKV_DIM), 1.0),
        'state_conf_conv': nrm(ks[4], (DEPTH, DEC_BATCH, CONF_CONV - 1, CONF_DIM), 0.5),
        'state_ffn_conv': nrm(ks[5], (DEPTH, DEC_BATCH, FFN_CONV - 1, 2 * D_FF), 1.0),
        'norm_mix_w': gain(ks[6], (DEPTH, D_MODEL)),
        'w_in': nrm(ks[7], (DEPTH, D_MODEL, IN_COLS), D_MODEL ** -0.5),
        'conv_qkv_w': nrm(ks[8], (DEPTH, SHORT_CONV, QKV_DIM), SHORT_CONV ** -0.5),
        'a_log': jnp.log(jax.random.uniform(ks[9], (DEPTH, N_HEADS), f32, 1.0, 16.0)),
        'dt_bias': dt + jnp.log(-jnp.expm1(-dt)),
        'delta_norm_w': gain(ks[11], (DEPTH, HEAD_DIM_V)),
        'w_o_delta': nrm(ks[12], (DEPTH, V_DIM, D_MODEL), V_DIM ** -0.5),
        'conf_conv_w': nrm(ks[13], (DEPTH, CONF_CONV, CONF_DIM), CONF_CONV ** -0.5),
        'conf_conv_b': nrm(ks[14], (DEPTH, CONF_DIM), 0.01),
        'conf_ln_w': gain(ks[15], (DEPTH, CONF_DIM)),
        'conf_ln_b': nrm(ks[16], (DEPTH, CONF_DIM), 0.01),
        'w_o_conf': nrm(ks[17], (DEPTH, CONF_DIM, D_MODEL), CONF_DIM ** -0.5),
        'w_out': nrm(ks[18], (DEPTH, D_MODEL, D_MODEL), D_MODEL ** -0.5),
        'norm_ffn_w': gain(ks[19], (DEPTH, D_MODEL)),
        'w_up': nrm(ks[20], (DEPTH, D_MODEL, 2 * D_FF), D_MODEL ** -0.5),
        'ffn_conv_w': nrm(ks[21], (DEPTH, FFN_CONV, 2 * D_FF), FFN_CONV ** -0.5),
        'ffn_conv_b': nrm(ks[22], (DEPTH, 2 * D_FF), 0.01),
        'w_down': nrm(ks[23], (DEPTH, D_FF, D_MODEL), D_FF ** -0.5),
        'norm_final_w': gain(ks[24], (D_MODEL,)),
    }


def reference(x_prompt, x_sample, state_delta, state_qkv_conv, state_conf_conv, state_ffn_conv,
              norm_mix_w, w_in, conv_qkv_w, a_log, dt_bias, delta_norm_w, w_o_delta,
              conf_conv_w, conf_conv_b, conf_ln_w, conf_ln_b, w_o_conf, w_out,
              norm_ffn_w, w_up, ffn_conv_w, ffn_conv_b, w_down, norm_final_w):
    b_p = x_prompt.shape[0]
    sd = x_prompt.dtype
    init_p = (jnp.zeros((b_p, N_HEADS, HEAD_DIM_K, HEAD_DIM_V), jnp.float32),
              jnp.zeros((b_p, SHORT_CONV - 1, QKV_DIM), sd),
              jnp.zeros((b_p, CONF_CONV - 1, CONF_DIM), sd),
              jnp.zeros((b_p, FFN_CONV - 1, 2 * D_FF), sd))
    xp, xs = x_prompt, x_sample
    new_p = ([], [], [], [])
    new_s = ([], [], [], [])
    for l in range(DEPTH):
        groups = ((xp, init_p, new_p),
                  (xs, (state_delta[l], state_qkv_conv[l], state_conf_conv[l], state_ffn_conv[l]), new_s))
        outs = []
        for x, st, sink in groups:
            h = rmsnorm(x, norm_mix_w[l])
            mix, n_delta, n_qkv, n_conf = token_mixer(
                h, st[0], st[1], st[2], w_in[l], conv_qkv_w[l], a_log[l], dt_bias[l], delta_norm_w[l],
                w_o_delta[l], conf_conv_w[l], conf_conv_b[l], conf_ln_w[l], conf_ln_b[l], w_o_conf[l], w_out[l])
            x = x + mix
            ff, n_ffn = conv_ffn(rmsnorm(x, norm_ffn_w[l]), st[3], w_up[l], ffn_conv_w[l], ffn_conv_b[l], w_down[l])
            x = x + ff
            for lst, val in zip(sink, (n_delta, n_qkv, n_conf, n_ffn)):
                lst.append(val)
            outs.append(x)
        xp, xs = outs[0], outs[1]
    y_prompt = rmsnorm(xp, norm_final_w)
    y_sample = rmsnorm(xs, norm_final_w)
    return (y_prompt, y_sample,
            jnp.stack(new_p[0]), jnp.stack(new_p[1]), jnp.stack(new_p[2]), jnp.stack(new_p[3]),
            jnp.stack(new_s[0]), jnp.stack(new_s[1]), jnp.stack(new_s[2]), jnp.stack(new_s[3]))
```

```python
import numpy as np
import concourse.bass as bass
import concourse.mybir as mybir

F32 = mybir.dt.float32
BF16 = mybir.dt.bfloat16
F32R = mybir.dt.float32r
AF = mybir.ActivationFunctionType
ALU = mybir.AluOpType
AX = mybir.AxisListType


class _Buf:
    __slots__ = ("writers", "readers")

    def __init__(self):
        self.writers = []
        self.readers = []


class _Op:
    __slots__ = ("eng", "fn", "deps", "dma", "marked", "token", "idx")


class Sched:
    INORDER = ("pe", "dve", "act", "pool")

    def __init__(self, nc, n_dma_sems=40):
        self.nc = nc
        self.ops = []
        self.bufs = {}
        self.engs = {"pe": nc.tensor, "dve": nc.vector, "act": nc.scalar,
                     "pool": nc.gpsimd, "sp": nc.sync}
        self.n_dma_sems = n_dma_sems

    def _buf(self, ap):
        name = ap.tensor.name if hasattr(ap, "tensor") else ap.name
        b = self.bufs.get(name)
        if b is None:
            b = self.bufs[name] = _Buf()
        return b

    def add(self, eng, fn, reads=(), writes=(), dma=False):
        op = _Op()
        op.eng, op.fn, op.dma, op.marked, op.token = eng, fn, dma, False, None
        op.idx = len(self.ops)
        deps = set()
        rb = []
        for a in reads:
            if a is None or isinstance(a, (int, float)):
                continue
            b = self._buf(a)
            if b not in rb:
                rb.append(b)
        wb = []
        for a in writes:
            if a is None:
                continue
            b = self._buf(a)
            if b not in wb:
                wb.append(b)
        for a in reads:
            if a is None or isinstance(a, (int, float)):
                continue
            nm = a.tensor.name if hasattr(a, "tensor") else a.name
            if nm.startswith("bank"):
                deps.update(self._buf(a).readers)
        for b in rb:
            deps.update(b.writers)
        for b in wb:
            deps.update(b.readers)
            deps.update(b.writers)
        for b in rb:
            if b not in wb:
                b.readers.append(op.idx)
        for b in wb:
            if b.readers or len(b.writers) > 64:
                b.writers = [op.idx]
                b.readers = []
            else:
                b.writers.append(op.idx)
        red = {}
        for j in deps:
            o = self.ops[j]
            if o.dma:
                red[("dma", j)] = j
            else:
                k = o.eng
                if k not in red or red[k] < j:
                    red[k] = j
        op.deps = sorted(red.values())
        self.ops.append(op)
        return op

    def emit(self, final_wait_eng="sp"):
        nc = self.nc
        ops = self.ops
        for op in ops:
            for j in op.deps:
                o = ops[j]
                if (not o.dma) and o.eng == "pe" and op.eng == "pe" and not op.dma:
                    continue
                o.marked = True
        sems = {k: nc.alloc_semaphore("sq_" + k) for k in self.INORDER}
        dsems = [nc.alloc_semaphore("sd_%d" % i) for i in range(self.n_dma_sems)]
        dcount = [0] * self.n_dma_sems
        counts = {k: 0 for k in self.INORDER}
        seen = {k: {} for k in self.engs}
        ndma = 0
        for op in ops:
            e = self.engs[op.eng]
            sn = seen[op.eng]
            waits = []
            for j in op.deps:
                o = ops[j]
                if (not o.dma) and o.eng == "pe" and op.eng == "pe" and not op.dma:
                    continue
                waits.append(o.token)
            slot = None
            if op.dma:
                slot = ndma % self.n_dma_sems
                ndma += 1
                if dcount[slot] > 0:
                    waits.append((dsems[slot], dcount[slot] * 16, "d%d" % slot))
            wmax = {}
            for (s, v, key) in waits:
                if key not in wmax or wmax[key][1] < v:
                    wmax[key] = (s, v)
            for key, (s, v) in wmax.items():
                if sn.get(key, 0) >= v:
                    continue
                sn[key] = v
                e.wait_ge(s, v)
            ins = op.fn(e)
            if op.dma:
                dcount[slot] += 1
                ins.then_inc(dsems[slot], 16)
                op.token = (dsems[slot], dcount[slot] * 16, "d%d" % slot)
            elif op.marked:
                counts[op.eng] += 1
                ins.then_inc(sems[op.eng], 1)
                op.token = (sems[op.eng], counts[op.eng], op.eng)
            else:
                op.token = (sems[op.eng], counts[op.eng] + 0, op.eng)
        e = self.engs[final_wait_eng]
        for slot in range(self.n_dma_sems):
            if dcount[slot] > 0:
                e.wait_ge(dsems[slot], dcount[slot] * 16)
        for k in self.INORDER:
            if counts[k] > 0:
                e.wait_ge(sems[k], counts[k])
        self.stats = dict(n_ops=len(ops), counts=counts, ndma=ndma)

    def mm(self, out, lhsT, rhs, start=True, stop=True):
        return self.add("pe", lambda e: e.matmul(out, lhsT, rhs, start=start, stop=stop),
                        reads=[lhsT, rhs], writes=[out])

    def tr(self, out, in_, ident):
        return self.add("pe", lambda e: e.transpose(out, in_, ident), reads=[in_, ident], writes=[out])

    def act(self, out, in_, func, bias=0.0, scale=1.0, accum_out=None):
        kw = {}
        if accum_out is not None:
            kw["accum_out"] = accum_out
        return self.add("act", lambda e: e.activation(out, in_, func, bias=bias, scale=scale, **kw),
                        reads=[in_, bias if not isinstance(bias, float) else None,
                               scale if not isinstance(scale, float) else None],
                        writes=[out, accum_out])

    def ts(self, out, in0, s1, s2=None, op0=ALU.mult, op1=None, eng="dve", accum_out=None):
        kw = {}
        if op1 is not None:
            kw["op1"] = op1
        if accum_out is not None:
            kw["accum_out"] = accum_out
        return self.add(eng, lambda e: e.tensor_scalar(out, in0, s1, s2, op0=op0, **kw),
                        reads=[in0, s1 if not isinstance(s1, (int, float)) else None,
                               s2 if not isinstance(s2, (int, float)) else None],
                        writes=[out, accum_out])

    def tt(self, out, in0, in1, op, eng="dve"):
        return self.add(eng, lambda e: e.tensor_tensor(out, in0, in1, op), reads=[in0, in1], writes=[out])

    def stt(self, out, in0, scalar, in1, op0, op1):
        return self.add("dve", lambda e: e.scalar_tensor_tensor(out, in0, scalar, in1, op0=op0, op1=op1),
                        reads=[in0, in1, scalar if not isinstance(scalar, (int, float)) else None],
                        writes=[out])

    def cp(self, out, in_, eng="dve"):
        if eng == "act":
            return self.add("act", lambda e: e.copy(out, in_), reads=[in_], writes=[out])
        return self.add(eng, lambda e: e.tensor_copy(out, in_), reads=[in_], writes=[out])

    def recip(self, out, in_):
        return self.add("dve", lambda e: e.reciprocal(out, in_), reads=[in_], writes=[out])

    def memset(self, ap, val, eng="dve"):
        return self.add(eng, lambda e: e.memset(ap, val), writes=[ap])

    def dma(self, out, in_, eng="sp", **kw):
        return self.add(eng, lambda e: e.dma_start(out, in_, **kw), reads=[in_], writes=[out], dma=True)

import numpy as np
from concourse.bass_utils import run_bass_kernel_spmd

from contextlib import ExitStack
import os

D = 1024
T = 2048
NS = 16
TT = T + NS
NH = 8
QKV = 3072
IN_COLS = 7184
DFF = 2816
EPS = 1e-6
GROUPS = [(0, 512), (512, 512), (1024, 512), (1536, 512), (2048, 16)]
TILES = [(t * 128, 128) for t in range(16)] + [(2048, 16)]
C_Z = 3072
C_BA = 4096
C_GLU = 4112
C_GA = 5136
C_GB = 6160
NEG = -30000.0


class _Stop(Exception):
    pass


def build_program(depth=2, upto=99):
    nc = bass.Bass("TRN2", target_bir_lowering=False)
    s = Sched(nc)
    uid = [0]

    def din(name, shape):
        return nc.dram_tensor(name, list(shape), F32, kind="ExternalInput").ap()

    def dout(name, shape):
        return nc.dram_tensor(name, list(shape), F32, kind="ExternalOutput").ap()

    xp = din("xp", [T, D]); xs = din("xs", [NS, D])
    sdelta = din("sdelta", [2, NS, NH, 128, 128])
    sqkv = din("sqkv", [2, NS, 3, QKV]); sconf = din("sconf", [2, NS, 30, 512]); sffn = din("sffn", [2, NS, 2, 2 * DFF])
    norm_mix_w = din("norm_mix_w", [2, D]); w_in = din("w_in", [2, D, IN_COLS]); conv_qkv_w = din("conv_qkv_w", [2, 4, QKV])
    a_log = din("a_log", [2, NH]); dt_bias = din("dt_bias", [2, NH]); delta_norm_w = din("delta_norm_w", [2, 128])
    w_o_delta = din("w_o_delta", [2, D, D]); conf_conv_w = din("conf_conv_w", [2, 31, 512]); conf_conv_b = din("conf_conv_b", [2, 512])
    conf_ln_w = din("conf_ln_w", [2, 512]); conf_ln_b = din("conf_ln_b", [2, 512]); w_o_conf = din("w_o_conf", [2, 512, D])
    w_out = din("w_out", [2, D, D]); norm_ffn_w = din("norm_ffn_w", [2, D]); w_up = din("w_up", [2, D, 2 * DFF])
    ffn_conv_w = din("ffn_conv_w", [2, 3, 2 * DFF]); ffn_conv_b = din("ffn_conv_b", [2, 2 * DFF]); w_down = din("w_down", [2, DFF, D])
    norm_final_w = din("norm_final_w", [D])

    yp = dout("yp", [T, D]); ys = dout("ys", [NS, D])
    nd_p = dout("nd_p", [2, NH, 128, 128]); nq_p = dout("nq_p", [2, 3, QKV]); nc_p = dout("nc_p", [2, 30, 512]); nf_p = dout("nf_p", [2, 2, 2 * DFF])
    nd_s = dout("nd_s", [2, NS, NH, 128, 128]); nq_s = dout("nq_s", [2, NS, 3, QKV]); nc_s = dout("nc_s", [2, NS, 30, 512]); nf_s = dout("nf_s", [2, NS, 2, 2 * DFF])

    def sb(name, shape, dt=F32):
        uid[0] += 1
        return nc.alloc_sbuf_tensor("%s_%d" % (name, uid[0]), list(shape), dt)

    def sbc(es, name, shape, dt=F32):
        uid[0] += 1
        return es.enter_context(nc.sbuf_tensor("%s_%d" % (name, uid[0]), list(shape), dt))

    banks = [nc.alloc_psum_tensor("bank%d" % i, [128, 512], F32) for i in range(8)]
    bk = [0]

    def pb():
        b = banks[bk[0] % 8]
        bk[0] += 1
        return b

    def bfv(bank):
        return bank[:].bitcast(BF16)

    idf = sb("idf", [128, 128]); idb = sb("idb", [128, 128], BF16)
    ones_f = sb("ones_f", [128, 128]); ones_b = sb("ones_b", [128, 128], BF16)
    triU = sb("triU", [128, 128]); negmask = sb("negmask", [128, 128]); strict01 = sb("strict01", [128, 128])
    X = sb("X", [128, 16, D]); XS = sb("XS", [16, D])
    HT = sb("HT", [128, 8, TT], BF16)
    ss = sb("ss", [128, 17]); rstd = sb("rstd", [128, 17])
    normw = sb("normw", [128, 8])

    def Xt(ti):
        t0, n = TILES[ti]
        return (X[:, ti, :] if ti < 16 else XS[:, :]), n

    bar1 = {k: sb("bar1_" + k, [128, 1]) for k in ("pe", "dve", "act", "pool", "sp")}
    bar2 = {k: sb("bar2_" + k, [128, 1]) for k in ("pe", "dve", "act", "pool", "sp")}

    def barrier():
        import os
        if os.environ.get("KNOBAR"):
            return
        which = os.environ.get("KBAR", "pe,dve,act,pool,sp").split(",")
        for (br, rd) in ((bar1, []), (bar2, [bar1[k][:] for k in bar1 if k in which])):
            b = pb()
            if "pe" in which:
                s.add("pe", lambda e, b=b: e.matmul(b[0:1, 0:2], ones_f[:, 0:1], ones_f[:, 0:2], start=True, stop=True),
                      reads=rd + [ones_f[:]], writes=[b[:], br["pe"][:]])
            if "dve" in which:
                s.add("dve", lambda e, br=br: e.memset(br["dve"][:], 0.0), reads=rd, writes=[br["dve"][:]])
            if "act" in which:
                s.add("act", lambda e, br=br: e.activation(br["act"][:], ones_f[:, 0:1], AF.Copy), reads=rd + [ones_f[:]], writes=[br["act"][:]])
            if "pool" in which:
                s.add("pool", lambda e, br=br: e.memset(br["pool"][:], 0.0), reads=rd, writes=[br["pool"][:]])
            if "sp" in which:
                op = s.add("sp", lambda e, br=br: e.dma_start(br["sp"][0:1, 0:1], ones_f[0:1, 0:1]), reads=rd + [ones_f[:]],
                           writes=[br["sp"][:]], dma=True)
                if not rd and not os.environ.get("KNOALLD"):
                    alld = [o.idx for o in s.ops if o.dma and o.idx > s.last_bar and o.idx != op.idx]
                    op.deps = sorted(set(op.deps) | set(alld))
                    s.last_bar = op.idx

    s.last_bar = -1

    def ckpt(k):
        if upto == k:
            barrier()
            raise _Stop()

    def wload(dst, src):
        s.dma(dst, src.rearrange("(k p) n -> p k n", p=128), eng="pool")

    def fm_vec(dst, src):
        s.dma(dst, src.rearrange("(k p) -> p k", p=128))

    es0 = ExitStack()
    with nc.allow_non_contiguous_dma(reason="small strided parameter loads"), nc.allow_low_precision(reason="bf16 matmuls"), es0:
        s.memset(idf[:], 0.0, eng="pool")
        s.add("pool", lambda e: e.affine_select(idf[:], idf[:], pattern=[[-1, 128]], compare_op=ALU.not_equal,
                                                  fill=1.0, base=0, channel_multiplier=1), reads=[idf[:]], writes=[idf[:]])
        s.cp(idb[:], idf[:])
        s.memset(ones_f[:], 1.0); s.memset(ones_b[:], 1.0)
        s.memset(triU[:], 1.0, eng="pool")
        s.add("pool", lambda e: e.affine_select(triU[:], triU[:], pattern=[[1, 128]], compare_op=ALU.is_ge,
                                                  fill=0.0, base=0, channel_multiplier=-1), reads=[triU[:]], writes=[triU[:]])
        s.memset(negmask[:], 0.0, eng="pool")
        s.add("pool", lambda e: e.affine_select(negmask[:], negmask[:], pattern=[[1, 128]], compare_op=ALU.is_ge,
                                                  fill=NEG, base=0, channel_multiplier=-1), reads=[negmask[:]], writes=[negmask[:]])
        s.memset(strict01[:], 1.0, eng="pool")
        s.add("pool", lambda e: e.affine_select(strict01[:], strict01[:], pattern=[[1, 128]], compare_op=ALU.is_ge,
                                                  fill=0.0, base=-1, channel_multiplier=-1), reads=[strict01[:]], writes=[strict01[:]])
        s.memset(ss[:], 1.0)
        for q4 in range(4):
            s.dma(X[:, q4 * 4:(q4 + 1) * 4, :], xp[q4 * 512:(q4 + 1) * 512, :].rearrange("(t p) d -> p t d", p=128))
        s.dma(XS[:, :], xs)

        def norm_to_HT(wrow):
            with ExitStack() as es:
                junk = sbc(es, "junk", [128, D], BF16)
                xb = sbc(es, "xb", [128, D], BF16)
                fm_vec(normw[:], wrow)
                for ti in range(17):
                    xa, n = Xt(ti)
                    s.act(junk[0:n, :], xa[0:n], AF.Square, accum_out=ss[0:n, ti:ti + 1])
                s.ts(rstd[:], ss[:], 1.0 / D, EPS, op0=ALU.mult, op1=ALU.add)
                s.act(rstd[:], rstd[:], AF.Sqrt)
                s.recip(rstd[:], rstd[:])
                xbs = [xb, junk]
                xa0, n0 = Xt(0)
                s.act(xbs[0][0:n0, :], xa0[0:n0], AF.Copy, scale=rstd[0:n0, 0:1])
                for ti in range(17):
                    xa, n = Xt(ti)
                    t0 = TILES[ti][0]
                    if ti + 1 < 17:
                        xa1, n1 = Xt(ti + 1)
                        s.act(xbs[(ti + 1) % 2][0:n1, :], xa1[0:n1], AF.Copy, scale=rstd[0:n1, ti + 1:ti + 2])
                    bank = pb()
                    pv = bfv(bank).rearrange("p (k t) -> p k t", k=8)
                    for k in range(8):
                        s.tr(pv[:, k, 0:n], xbs[ti % 2][0:n, k * 128:(k + 1) * 128], idb[0:n, 0:n])
                    s.tt(HT[:, :, t0:t0 + n], pv[:, :, 0:n], normw[:].unsqueeze(2).broadcast_to([128, 8, n]), ALU.mult)
                barrier()

        def proj(bank, W, kn, src, g0, gw):
            for k in range(kn):
                s.mm(bank[:, 0:gw], W[:, k, :], src[:, k, g0:g0 + gw], start=(k == 0), stop=(k == kn - 1))

        try:
          for l in range(depth):
            ckpt(1)
            norm_to_HT(norm_mix_w[l])
            ckpt(2)
            esL = ExitStack()
            SQT = sbc(esL, "SQT", [128, 24, NS, 3], BF16)
            TAILQ = sbc(esL, "TAILQ", [128, 24, 19])
            BETA = sbc(esL, "BETA", [128, 17, 8]); GC = sbc(esL, "GC", [128, 17, 8]); EGC = sbc(esL, "EGC", [128, 17, 8])
            KSC2 = sbc(esL, "KSC2", [128, 16, 8]); GLAST = sbc(esL, "GLAST", [128, 16, 8])
            BETAS = sbc(esL, "BETAS", [128, NS, 8]); EGS = sbc(esL, "EGS", [128, NS, 8])
            OG = sbc(esL, "OG", [128, 8, TT], BF16)
            cw = sbc(esL, "cw", [128, 24, 4]); dnw = sbc(esL, "dnw", [128, 1])
            for j in range(4):
                s.dma(cw[:, :, j], conv_qkv_w[l, j].rearrange("(c p) -> p c", p=128))
            s.dma(dnw[:, 0:1], delta_norm_w[l].rearrange("(p o) -> p o", o=1))
            s.dma(nq_s[l, :, 0:2, :], sqkv[l, :, 1:3, :])
            s.dma(nc_s[l, :, 0:29, :], sconf[l, :, 1:30, :])
            s.dma(nf_s[l, :, 0:1, :], sffn[l, :, 1:2, :])
            with ExitStack() as es:
                ckpt(21)
                ST = sbc(es, "ST", [48, QKV])
                s.dma(ST[:, :], sqkv[l].rearrange("s j c -> (s j) c"))
                for c4 in range(6):
                    bank = pb()
                    for cc in range(4):
                        c = c4 * 4 + cc
                        s.tr(bank[:, cc * 48:(cc + 1) * 48], ST[0:48, c * 128:(c + 1) * 128], idf[0:48, 0:48])
                    s.cp(SQT[:, c4 * 4:(c4 + 1) * 4, :, :].rearrange("p c s j -> p (c s j)"), bank[:, 0:192], eng="act")
                ckpt(22)
                Wba = sbc(es, "Wba", [128, 8, 16], BF16)
                wload(Wba[:], w_in[l, :, C_BA:C_BA + 16])
                BA = sbc(es, "BA", [128, 17, 16]); T1 = sbc(es, "T1", [128, 17, 8]); G = sbc(es, "G", [128, 17, 8])
                GCL = sbc(es, "GCL", [128, 16, 8])
                dtb = sbc(es, "dtb", [128, 8]); negA = sbc(es, "negA", [128, 8])
                s.dma(dtb[:], dt_bias[l:l + 1, :].broadcast_to([128, 8]))
                s.dma(negA[:], a_log[l:l + 1, :].broadcast_to([128, 8]))
                s.act(negA[:], negA[:], AF.Exp)
                s.ts(negA[:], negA[:], -1.0, None, op0=ALU.mult)
                s.memset(BA[:], 0.0)
                bank = pb()
                for ti in range(17):
                    t0, n = TILES[ti]
                    for k in range(8):
                        s.mm(bank[0:n, ti * 16:(ti + 1) * 16], HT[:, k, t0:t0 + n], Wba[:, k, :], start=(k == 0), stop=(k == 7))
                s.cp(BA[:, 0:16, :].rearrange("p t c -> p (t c)"), bank[:, 0:256])
                s.cp(BA[0:16, 16, :], bank[0:16, 256:272])
                ckpt(23)
                s.act(BETA[:], BA[:, :, 0:8], AF.Sigmoid)
                s.tt(T1[:], BA[:, :, 8:16], dtb[:].unsqueeze(1).broadcast_to([128, 17, 8]), ALU.add)
                UE = sbc(es, "UE", [128, 17, 8]); UC = sbc(es, "UC", [128, 17, 8]); RP = sbc(es, "RP", [128, 17, 8]); MK = sbc(es, "MK", [128, 17, 8])
                s.act(UE[:], T1[:], AF.Exp)
                s.act(T1[:], UE[:], AF.Ln, bias=1.0)
                s.ts(UC[:], UE[:], 0.5, None, op0=ALU.min)
                s.ts(RP[:], UC[:], -1.0 / 8, None, op0=ALU.mult)
                for cf in (1.0 / 7, -1.0 / 6, 1.0 / 5, -1.0 / 4, 1.0 / 3, -1.0 / 2, 1.0):
                    s.stt(RP[:], RP[:], cf, UC[:], op0=ALU.add, op1=ALU.mult)
                s.ts(MK[:], UE[:], 0.35, None, op0=ALU.is_lt)
                s.tt(RP[:], RP[:], T1[:], ALU.subtract)
                s.tt(RP[:], RP[:], MK[:], ALU.mult)
                s.tt(T1[:], T1[:], RP[:], ALU.add)
                s.tt(G[:], T1[:], negA[:].unsqueeze(1).broadcast_to([128, 17, 8]), ALU.mult)
                ckpt(25)
                bank = pb()
                Gp = G[:, 0:16, :].rearrange("p t h -> p (t h)")
                bank2 = pb()
                s.mm(bank[:, 0:128], triU[:], Gp)
                s.cp(GC[:, 0:16, :].rearrange("p t h -> p (t h)"), bank[:, 0:128])
                s.mm(bank2[:, 0:128], ones_f[:], Gp)
                s.cp(GCL[:].rearrange("p t h -> p (t h)"), bank2[:, 0:128])
                s.cp(GC[0:16, 16, :], G[0:16, 16, :])
                ckpt(26)
                s.act(EGC[:], GC[:], AF.Exp)
                s.tt(KSC2[:], GCL[:], GC[:, 0:16, :], ALU.subtract)
                s.act(KSC2[:], KSC2[:], AF.Exp)
                s.act(GLAST[:], GCL[:], AF.Exp)
                ckpt(24)
                BD = sbc(es, "BD", [16, NS, 8])
                for (src, dst) in ((BETA, BETAS), (EGC, EGS)):
                    s.tt(BD[:], src[0:16, 16, :].unsqueeze(1).broadcast_to([16, NS, 8]),
                         idf[0:16, 0:16].unsqueeze(2).broadcast_to([16, NS, 8]), ALU.mult)
                    bank = pb()
                    s.mm(bank[:, 0:128], ones_f[0:16, :], BD[:].rearrange("p s h -> p (s h)"))
                    s.cp(dst[:].rearrange("p s h -> p (s h)"), bank[:, 0:128])
                barrier()

            ckpt(3)
            with ExitStack() as es:
                Wq = [sbc(es, "Wq%d" % i, [128, 8, 128], BF16) for i in range(4)]
                PRE = [sbc(es, "PRE%d" % i, [128, 515], BF16) for i in range(3)]
                PREs = sbc(es, "PREs", [128, 3, NS], BF16)
                DGq = sbc(es, "DGq", [128, 3, 4, 128], BF16)
                QKd = [[sbc(es, "QKg%d_%d" % (i, k), [128, 512], BF16) for i in range(3)] for k in range(2)]
                SQB = sbc(es, "SQB", [128, 512], BF16); R1 = sbc(es, "R1", [128, 512])
                SQB2 = sbc(es, "SQB2", [128, 512], BF16); R1b = sbc(es, "R1b", [128, 512]); TMPn = sbc(es, "TMPn", [128, 512], BF16)
                SQB3 = sbc(es, "SQB3", [128, NS], BF16); R1c = sbc(es, "R1c", [128, NS]); TMPs = sbc(es, "TMPs", [128, NS]); OTs = sbc(es, "OTs", [128, NS])
                ZSs = sbc(es, "ZSs", [128, NS], BF16)
                DIAG4 = sbc(es, "DIAG4", [128, 4, 128]); DTm4 = sbc(es, "DTm4", [128, 4, 128])
                DECI4 = sbc(es, "DECI4", [128, 4, 128], BF16); EGB4 = sbc(es, "EGB4", [128, 4, 128], BF16)
                DECS4 = EGB4
                MLp = [sbc(es, "MLp%d" % i, [128, 2, 2, 128]) for i in range(2)]
                PPp = [sbc(es, "PPp%d" % i, [128, 2, 128]) for i in range(2)]
                KGC = [sbc(es, "KGC%d" % i, [128, 4, 128], BF16) for i in range(2)]
                KG = [sbc(es, "KG%d" % i, [128, 4, 128], BF16) for i in range(2)]
                VT = [sbc(es, "VT%d" % i, [128, 4, 128], BF16) for i in range(2)]
                QG = [sbc(es, "QG%d" % i, [128, 4, 128], BF16) for i in range(2)]
                QKT = [sbc(es, "QKT%d" % i, [128, 4, 128], BF16) for i in range(2)]
                Pb = [sbc(es, "Pb%d" % i, [128, 4, 128], BF16) for i in range(2)]
                NWT = [sbc(es, "NWT%d" % i, [128, 4, 128], BF16) for i in range(2)]
                ZSg = [sbc(es, "ZSg%d" % i, [128, 512], BF16) for i in range(3)]
                VN = sbc(es, "VN", [128, 128], BF16)
                S = sbc(es, "S", [128, 128]); Sb = sbc(es, "Sb", [128, 128], BF16)
                OTg = sbc(es, "OTg", [128, 512], BF16)
                QKs = sbc(es, "QKs", [128, 3, NS])
                SDd = [sbc(es, "SD%d" % k, [128, 1, 128]) for k in range(2)]
                SDnd = [sbc(es, "SDn%d" % k, [128, 1, 128]) for k in range(2)]
                VNS = sbc(es, "VNS", [128, NS]); TS1 = sbc(es, "TS1", [128, NS])
                D1 = sbc(es, "D1", [128, 128]); TM2 = sbc(es, "TM2", [128, 128])
                bk1 = [0]; bkA = [0]; bkB = [0]

                def pb1():
                    bk1[0] += 1
                    return banks[bk1[0] % 2]

                def pbA():
                    bkA[0] += 1
                    return banks[2 + bkA[0] % 3]

                def pbB():
                    bkB[0] += 1
                    return banks[5 + bkB[0] % 2]

                def pbS():
                    return banks[7]

                def rstd_of(dst, src, scale, bias):
                    s.act(dst, src, AF.Ln, scale=scale, bias=bias)
                    s.act(dst, dst, AF.Exp, scale=-0.5)

                def l2norm_a(buf, n, sq):
                    s.act(sq[:, 0:n], buf, AF.Square)

                def l2norm_b(buf, n, qscale, sq, r1, pbf):
                    bank = pbf()
                    s.mm(bank[:, 0:n], ones_b[:], sq[:, 0:n])
                    if qscale:
                        rstd_of(r1[:, 0:n], bank[:, 0:n], 128.0, 128.0 * EPS)
                    else:
                        rstd_of(r1[:, 0:n], bank[:, 0:n], 1.0, EPS)
                    s.tt(buf, buf, r1[:, 0:n], ALU.mult)

                def l2norm(buf, n, qscale, sq, r1, pbf):
                    s.act(sq[:, 0:n], buf, AF.Square)
                    bank = pbf()
                    s.mm(bank[:, 0:n], ones_b[:], sq[:, 0:n])
                    if qscale:
                        rstd_of(r1[:, 0:n], bank[:, 0:n], 128.0, 128.0 * EPS)
                    else:
                        rstd_of(r1[:, 0:n], bank[:, 0:n], 1.0, EPS)
                    s.tt(buf, buf, r1[:, 0:n], ALU.mult)

                def gated_norm_a(ot, n, sq):
                    s.act(sq[:, 0:n], ot, AF.Square)

                def gated_norm_b(h, ot, zs, g0, n, sq, r1, tmp, pbf):
                    bank = pbf()
                    s.mm(bank[:, 0:n], ones_b[:], sq[:, 0:n])
                    rstd_of(r1[:, 0:n], bank[:, 0:n], 1.0 / 128, EPS)
                    s.tt(tmp[:, 0:n], ot, r1[:, 0:n], ALU.mult)
                    s.stt(OG[:, h, g0:g0 + n], tmp[:, 0:n], dnw[:, 0:1], zs, op0=ALU.mult, op1=ALU.mult)

                def gated_norm(h, ot, zs, g0, n, sq, r1, tmp, pbf):
                    s.act(sq[:, 0:n], ot, AF.Square)
                    bank = pbf()
                    s.mm(bank[:, 0:n], ones_b[:], sq[:, 0:n])
                    rstd_of(r1[:, 0:n], bank[:, 0:n], 1.0 / 128, EPS)
                    s.tt(tmp[:, 0:n], ot, r1[:, 0:n], ALU.mult)
                    s.stt(OG[:, h, g0:g0 + n], tmp[:, 0:n], dnw[:, 0:1], zs, op0=ALU.mult, op1=ALU.mult)

                sfront = [True]
                sgen = [None]

                def bg_step():
                    if sgen[0] is not None:
                        try:
                            next(sgen[0])
                        except StopIteration:
                            sgen[0] = None

                def load_wq(h):
                    cols = [h * 128, 1024 + h * 128, 2048 + h * 128, C_Z + h * 128]
                    for i in range(4):
                        wload(Wq[i][:], w_in[l, :, cols[i]:cols[i] + 128])

                def genA1(u):
                    h, g = divmod(u, 4)
                    QK = QKd[u % 2]
                    g0 = g * 512
                    if g == 0:
                        if h == 0:
                            load_wq(0)
                        for ci in range(3):
                            cq = ci * 8 + h
                            for j in range(4):
                                s.ts(DGq[:, ci, j, :], idb[:], cw[:, cq, j:j + 1], None, op0=ALU.mult)
                            s.memset(PRE[ci][:, 0:3], 0.0)
                        yield
                    bz = pb1()
                    proj(bz, Wq[3], 8, HT, g0, 512)
                    s.act(ZSg[u % 3][:, :], bz[:, :], AF.Silu)
                    for ci in range(3):
                        cq = ci * 8 + h
                        bank = pb1()
                        proj(bank, Wq[ci], 8, HT, g0, 512)
                        if g > 0:
                            s.cp(PRE[ci][:, 0:3], PRE[ci][:, 512:515])
                        s.cp(PRE[ci][:, 3:515], bank[:, :], eng=("act" if ci != 1 else "dve"))
                        if g == 3:
                            s.cp(TAILQ[:, cq, 0:3], bank[:, 509:512])
                        if ci != 1:
                            yield
                    for ci in range(3):
                        bank = pb1()
                        for j in range(4):
                            s.mm(bank[:, :], DGq[:, ci, j, :], PRE[ci][:, j:j + 512], start=(j == 0), stop=(j == 3))
                        s.act(QK[ci][:, :], bank[:, :], AF.Silu)
                        if ci == 1:
                            yield
                    if g == 3 and h + 1 < NH:
                        while not sfront[0]:
                            bg_step()
                        load_wq(h + 1)
                    l2norm_a(QK[0][:, :], 512, SQB)
                    yield
                    l2norm_b(QK[0][:, :], 512, True, SQB, R1, pb1)
                    l2norm_a(QK[1][:, :], 512, SQB)
                    yield
                    l2norm_b(QK[1][:, :], 512, False, SQB, R1, pb1)
                    yield

                def genA2(u):
                    h, g = divmod(u, 4)
                    par = u % 2
                    QK = QKd[par]
                    g0 = g * 512
                    c0 = g * 4
                    bc = lambda t: t[:, c0:c0 + 4, h:h + 1].broadcast_to([128, 4, 128])
                    s.tt(DIAG4[:], idf[:].unsqueeze(1).broadcast_to([128, 4, 128]), bc(GC), ALU.mult)
                    bank = pbA(); pv = bfv(bank)
                    for cc in range(4):
                        cs = slice(cc * 128, (cc + 1) * 128)
                        s.tr(pv[:, cc * 128:(cc + 1) * 128], QK[1][:, cs], idb[:])
                        s.tr(pv[:, 512 + cc * 128:512 + (cc + 1) * 128], QK[2][:, cs], idb[:])
                    bankG = pbA()
                    s.mm(bankG[:, :], ones_f[:], DIAG4[:].rearrange("p c n -> p (c n)"))
                    bKK = pbA()
                    for cc in range(4):
                        cs = slice(cc * 128, (cc + 1) * 128)
                        s.mm(bKK[:, cs], QK[1][:, cs], QK[1][:, cs])
                    pk = pv[:, 0:512].rearrange("p (c n) -> p c n", c=4)
                    s.tt(KGC[par][:], pk, bc(EGC), ALU.mult)
                    s.tt(KG[par][:], pk, bc(KSC2), ALU.mult)
                    s.cp(VT[par][:].rearrange("p c n -> p (c n)"), pv[:, 512:1024], eng="act")
                    for cc in range(4):
                        s.stt(DTm4[:, cc, :], bankG[:, cc * 128:(cc + 1) * 128], GC[:, c0 + cc, h:h + 1], negmask[:],
                              op0=ALU.subtract, op1=ALU.add)
                    s.act(EGB4[:].rearrange("p c n -> p (c n)"), bankG[:, :], AF.Exp)
                    s.act(DECI4[:], DTm4[:], AF.Exp)
                    yield
                    bKQ = pbA()
                    for cc in range(4):
                        cs = slice(cc * 128, (cc + 1) * 128)
                        s.mm(bKQ[:, cs], QK[1][:, cs], QK[0][:, cs])
                    s.tt(QG[par][:].rearrange("p c n -> p (c n)"), QK[0][:, :], EGB4[:].rearrange("p c n -> p (c n)"), ALU.mult)
                    s.tt(DECS4[:], DECI4[:], strict01[:].unsqueeze(1).broadcast_to([128, 4, 128]), ALU.mult, eng="pool")
                    s.tt(DTm4[:], bKK[:, :].rearrange("p (c n) -> p c n", c=4), bc(BETA), ALU.mult)
                    for pr in range(2):
                        s.tt(MLp[pr][:, :, 0, :], DTm4[:, 2 * pr:2 * pr + 2, :], DECS4[:, 2 * pr:2 * pr + 2, :], ALU.mult)
                    s.tt(QKT[par][:], bKQ[:, :].rearrange("p (c n) -> p c n", c=4), DECI4[:], ALU.mult)
                    yield
                    for pr in range(2):
                        bank = pbA()
                        for i2 in range(2):
                            s.tr(bank[:, i2 * 128:(i2 + 1) * 128], MLp[pr][:, i2, 0, :], idf[:])
                        s.cp(MLp[pr][:, :, 1, :], bank[:, 0:256].rearrange("p (c n) -> p c n", c=2), eng=("act" if pr == 0 else "dve"))
                        s.tt(PPp[pr][:], idf[:].unsqueeze(1).broadcast_to([128, 2, 128]), MLp[pr][:, :, 0, :], ALU.subtract)
                        yield
                    for lev in range(1, 7):
                        for pr in range(2):
                            m = MLp[pr]
                            b1 = pbA()
                            for i2 in range(2):
                                if lev < 6:
                                    s.mm(b1[:, i2 * 256:i2 * 256 + 128], m[:, i2, 1, :], m[:, i2, 0, :])
                                s.mm(b1[:, i2 * 256 + 128:i2 * 256 + 256], m[:, i2, 0, :], m[:, i2, 1, :])
                            src = b1[:, :].rearrange("p (c t n) -> p c t n", c=2, t=2)
                            if lev < 6:
                                s.cp(m[:, :, :, :], src, eng=("act" if pr == 0 else "dve"))
                            else:
                                s.cp(m[:, :, 1, :], src[:, :, 1, :], eng=("act" if pr == 0 else "dve"))
                            yield
                        for pr in range(2):
                            m = MLp[pr]
                            b3 = pbA()
                            for i2 in range(2):
                                s.mm(b3[:, i2 * 128:(i2 + 1) * 128], m[:, i2, 1, :], PPp[pr][:, i2, :])
                            pa = PPp[pr][:].rearrange("p c n -> p (c n)")
                            s.tt(pa, b3[:, 0:256], pa, ALU.add)
                            yield
                    for pr in range(2):
                        s.cp(Pb[par][:, 2 * pr:2 * pr + 2, :], PPp[pr][:], eng=("act" if pr == 0 else "dve"))
                    yield
                    bank = pbA()
                    for cc in range(4):
                        s.mm(bank[:, cc * 128:(cc + 1) * 128], KGC[par][:, cc, :], Pb[par][:, cc, :])
                    s.ts(NWT[par][:].rearrange("p c n -> p (c n)"), bank[:, :], -1.0, None, op0=ALU.mult)
                    yield

                def genB(u):
                    h, g = divmod(u, 4)
                    par = u % 2
                    g0 = g * 512
                    if g == 0:
                        s.memset(S[:], 0.0); s.memset(Sb[:], 0.0)
                    for cc in range(4):
                        c = g * 4 + cc
                        cs = slice(cc * 128, (cc + 1) * 128)
                        bA = pbB()
                        s.mm(bA[:, 0:128], Pb[par][:, cc, :], VT[par][:, cc, :], start=True, stop=False)
                        s.mm(bA[:, 0:128], NWT[par][:, cc, :], Sb[:], start=False, stop=True)
                        s.ts(VN[:], bA[:, 0:128], BETA[:, c, h:h + 1], None, op0=ALU.mult)
                        yield
                        bC = pbB()
                        s.mm(bC[:, 0:128], Sb[:], QG[par][:, cc, :], start=True, stop=False)
                        s.mm(bC[:, 0:128], VN[:], QKT[par][:, cc, :], start=False, stop=True)
                        s.cp(OTg[:, cs], bC[:, 0:128], eng="act")
                        bD = pbB()
                        s.mm(bD[:, 0:128], KG[par][:, cc, :], VN[:])
                        s.stt(S[:], S[:], GLAST[:, c, h:h + 1], bD[:, 0:128], op0=ALU.mult, op1=ALU.add)
                        s.cp(Sb[:], S[:], eng="act")
                        yield
                    gated_norm_a(OTg[:, :], 512, SQB2)
                    yield
                    gated_norm_b(h, OTg[:, :], ZSg[u % 3][:, :], g0, 512, SQB2, R1b, TMPn, pbB)
                    if g == 3:
                        s.dma(nd_p[l, h], S[:])
                    yield

                def genS(h):
                    bank = pbS()
                    proj(bank, Wq[3], 8, HT, T, NS)
                    s.act(ZSs[:, :], bank[:, 0:NS], AF.Silu)
                    yield
                    for ci in range(3):
                        cq = ci * 8 + h
                        bank = pbS()
                        proj(bank, Wq[ci], 8, HT, T, NS)
                        s.cp(PREs[:, ci, :], bank[:, 0:NS], eng="act")
                        s.cp(TAILQ[:, cq, 3:19], bank[:, 0:NS])
                        yield
                        bank = pbS()
                        for j in range(3):
                            s.mm(bank[:, 0:NS], DGq[:, ci, j, :], SQT[:, cq, :, j], start=(j == 0), stop=False)
                        s.mm(bank[:, 0:NS], DGq[:, ci, 3, :], PREs[:, ci, :], start=False, stop=True)
                        s.act(QKs[:, ci, :], bank[:, 0:NS], AF.Silu)
                        yield
                    sfront[0] = True
                    l2norm_a(QKs[:, 0, :], NS, SQB3)
                    yield
                    l2norm_b(QKs[:, 0, :], NS, True, SQB3, R1c, pbS)
                    l2norm_a(QKs[:, 1, :], NS, SQB3)
                    yield
                    l2norm_b(QKs[:, 1, :], NS, False, SQB3, R1c, pbS)
                    yield
                    s.dma(SDd[0][:], sdelta[l, 0:1, h].rearrange("s k v -> k s v"))
                    yield
                    for sm in range(NS):
                        SD_ = SDd[sm % 2]; SDn_ = SDnd[sm % 2]
                        bR = pbS()
                        s.mm(bR[:, 0:1], SD_[:, 0, :], QKs[:, 1, sm:sm + 1])
                        if sm + 1 < NS:
                            s.dma(SDd[(sm + 1) % 2][:], sdelta[l, sm + 1:sm + 2, h].rearrange("s k v -> k s v"))
                        s.tt(TS1[:, 0:1], bR[:, 0:1], EGS[:, sm, h:h + 1], ALU.mult)
                        s.tt(TS1[:, 0:1], QKs[:, 2, sm:sm + 1], TS1[:, 0:1], ALU.subtract)
                        s.tt(VNS[:, 0:1], TS1[:, 0:1], BETAS[:, sm, h:h + 1], ALU.mult)
                        s.ts(D1[:], idf[:], VNS[:, 0:1], None, op0=ALU.mult)
                        yield
                        bV = pbS()
                        s.mm(bV[:, 0:128], ones_f[:], D1[:])
                        s.ts(TM2[:], bV[:, 0:128], QKs[:, 1, sm:sm + 1], None, op0=ALU.mult)
                        s.stt(SDn_[:, 0, :], SD_[:, 0, :], EGS[:, sm, h:h + 1], TM2[:], op0=ALU.mult, op1=ALU.add)
                        yield
                        bO = pbS()
                        s.mm(bO[:, 0:1], SDn_[:, 0, :], QKs[:, 0, sm:sm + 1])
                        s.cp(OTs[:, sm:sm + 1], bO[:, 0:1])
                        s.dma(nd_s[l, sm:sm + 1, h].rearrange("s k v -> k s v"), SDn_[:])
                        yield
                    gated_norm_a(OTs[:, 0:NS], NS, SQB3)
                    yield
                    gated_norm_b(h, OTs[:, 0:NS], ZSs[:, :], T, NS, SQB3, R1c, TMPs, pbS)
                    yield

                def run_streams(gens, weights, bg=None):
                    live = [[g, w] for g, w in zip(gens, weights) if g is not None]
                    while live:
                        for item in list(live):
                            for _ in range(item[1]):
                                try:
                                    next(item[0])
                                except StopIteration:
                                    live.remove(item)
                                    break
                        if bg is not None and bg[1]:
                            try:
                                next(bg[0])
                            except StopIteration:
                                bg[1] = False

                def step_all(gens, weights):
                    live = [[g_, w_] for g_, w_ in zip(gens, weights) if g_ is not None]
                    while live:
                        for item in list(live):
                            for _ in range(item[1]):
                                try:
                                    next(item[0])
                                except StopIteration:
                                    live.remove(item)
                                    break
                        bg_step()

                NU = NH * 4
                for st in range(-2, NU):
                    a1 = st + 2
                    gA1 = genA1(a1) if a1 < NU else None
                    if gA1 is not None and a1 % 4 == 0:
                        while sgen[0] is not None:
                            bg_step()
                        next(gA1)
                        sfront[0] = False
                        sgen[0] = genS(a1 // 4)
                    step_all([gA1, genA2(st + 1) if 0 <= st + 1 < NU else None, genB(st) if st >= 0 else None], [1, 3, 1])
                while sgen[0] is not None:
                    bg_step()
                    ckpt(5)
                barrier()
            with ExitStack() as es:
                TQ = sbc(es, "TQ", [19, QKV])
                for c4 in range(6):
                    bank = pb()
                    for cc in range(4):
                        c = c4 * 4 + cc
                        s.tr(bank[0:19, cc * 128:(cc + 1) * 128], TAILQ[:, c, :], idf[:])
                    s.cp(TQ[:, c4 * 512:(c4 + 1) * 512], bank[0:19, :])
                s.dma(nq_p[l], TQ[0:3, :])
                s.dma(nq_s[l, :, 2, :], TQ[3:19, :])
                barrier()

            ckpt(6)
            CB = sbc(esL, "CB", [128, 4, TT], BF16)
            with ExitStack() as es:
                UH = sbc(es, "UH", [128, 4, NS, 30], BF16)
                with ExitStack() as es2:
                    ST2 = sbc(es2, "ST2", [120, 4, 512])
                    s.dma(ST2[:], sconf[l].rearrange("s j c -> (s j) c").rearrange("(r p) c -> p r c", p=120))
                    for r in range(4):
                        bank = pb()
                        for c in range(4):
                            s.tr(bank[:, c * 120:(c + 1) * 120], ST2[0:120, r, c * 128:(c + 1) * 128], idf[0:120, 0:120])
                        s.cp(UH[:, :, r * 4:(r + 1) * 4, :].rearrange("p c s j -> p c (s j)"),
                             bank[:, 0:480].rearrange("p (c x) -> p c x", c=4), eng="act")
                    barrier()
                ccw = sbc(es, "ccw", [128, 4, 31]); ccb = sbc(es, "ccb", [128, 4]); lnw = sbc(es, "lnw", [128, 4]); lnb = sbc(es, "lnb", [128, 4])
                for j in range(31):
                    s.dma(ccw[:, :, j], conf_conv_w[l, j].rearrange("(c p) -> p c", p=128))
                fm_vec(ccb[:], conf_conv_b[l]); fm_vec(lnw[:], conf_ln_w[l]); fm_vec(lnb[:], conf_ln_b[l])
                Wa = sbc(es, "Wa", [128, 8, 128], BF16); Wb = sbc(es, "Wb", [128, 8, 128], BF16)
                U1 = sbc(es, "U1", [128, 30 + TT], BF16)
                DGc = sbc(es, "DGc", [128, 31, 128], BF16)
                SG = sbc(es, "SG", [128, 512]); UF = sbc(es, "UF", [128, 512])
                UT = sbc(es, "UT", [128, 4, 46])
                s.memset(U1[:, 0:30], 0.0)
                for c in range(4):
                    wload(Wa[:], w_in[l, :, C_GLU + c * 128:C_GLU + (c + 1) * 128])
                    wload(Wb[:], w_in[l, :, C_GLU + 512 + c * 128:C_GLU + 512 + (c + 1) * 128])
                    for j in range(31):
                        s.ts(DGc[:, j, :], idb[:], ccw[:, c, j:j + 1], None, op0=ALU.mult)
                    for (g0, gw) in GROUPS:
                        b1 = pb(); b2 = pb()
                        proj(b1, Wa, 8, HT, g0, gw)
                        proj(b2, Wb, 8, HT, g0, gw)
                        s.act(SG[:, 0:gw], b2[:, 0:gw], AF.Sigmoid)
                        s.tt(UF[:, 0:gw], b1[:, 0:gw], SG[:, 0:gw], ALU.mult)
                        s.cp(U1[:, 30 + g0:30 + g0 + gw], UF[:, 0:gw], eng="act")
                        if g0 == 1536:
                            s.cp(UT[:, c, 0:30], UF[:, 482:512])
                        if g0 == T:
                            s.cp(UT[:, c, 30:46], UF[:, 0:NS])
                    for (g0, gw) in GROUPS:
                        bank = pb()
                        for j in range(31):
                            if g0 < T:
                                rhs = U1[:, g0 + j:g0 + j + gw]
                            else:
                                rhs = UH[:, c, :, j] if j < 30 else U1[:, 30 + T:30 + TT]
                            s.mm(bank[:, 0:gw], DGc[:, j, :], rhs, start=(j == 0), stop=(j == 30))
                        s.ts(CB[:, c, g0:g0 + gw], bank[:, 0:gw], ccb[:, c:c + 1], None, op0=ALU.add)
                TC = sbc(es, "TC", [46, 512])
                bank = pb()
                for c in range(4):
                    s.tr(bank[0:46, c * 128:(c + 1) * 128], UT[:, c, :], idf[:])
                s.cp(TC[:, :], bank[0:46, :])
                s.dma(nc_p[l], TC[0:30, :])
                s.dma(nc_s[l, :, 29, :], TC[30:46, :])
                SQc = sbc(es, "SQc", [128, 512], BF16)
                MEAN = SG; MSQ = UF; VAR = sbc(es, "VAR", [128, 512]); RS = sbc(es, "RS", [128, 512])
                TMPc = sbc(es, "TMPc", [128, 512])
                for (g0, gw) in GROUPS:
                    bm = pb(); bq = pb()
                    for c in range(4):
                        s.mm(bm[:, 0:gw], ones_b[:], CB[:, c, g0:g0 + gw], start=(c == 0), stop=(c == 3))
                    for c in range(4):
                        s.act(SQc[:, 0:gw], CB[:, c, g0:g0 + gw], AF.Square)
                        s.mm(bq[:, 0:gw], ones_b[:], SQc[:, 0:gw], start=(c == 0), stop=(c == 3))
                    s.ts(MEAN[:, 0:gw], bm[:, 0:gw], 1.0 / 512, None, op0=ALU.mult)
                    s.tt(MSQ[:, 0:gw], MEAN[:, 0:gw], MEAN[:, 0:gw], ALU.mult)
                    s.stt(VAR[:, 0:gw], bq[:, 0:gw], 1.0 / 512, MSQ[:, 0:gw], op0=ALU.mult, op1=ALU.subtract)
                    s.ts(VAR[:, 0:gw], VAR[:, 0:gw], 0.0, None, op0=ALU.max)
                    s.act(RS[:, 0:gw], VAR[:, 0:gw], AF.Sqrt, bias=EPS)
                    s.recip(RS[:, 0:gw], RS[:, 0:gw])
                    for c in range(4):
                        s.tt(TMPc[:, 0:gw], CB[:, c, g0:g0 + gw], MEAN[:, 0:gw], ALU.subtract)
                        s.tt(TMPc[:, 0:gw], TMPc[:, 0:gw], RS[:, 0:gw], ALU.mult)
                        s.act(CB[:, c, g0:g0 + gw], TMPc[:, 0:gw], AF.Silu, scale=lnw[:, c:c + 1], bias=lnb[:, c:c + 1])
                barrier()

            ckpt(7)
            with ExitStack() as es:
                MG = sbc(es, "MG", [128, 4, TT], BF16)
                W3 = [[sbc(es, "WgA%d" % k, [128, 8, 128], BF16), sbc(es, "WgB%d" % k, [128, 8, 128], BF16),
                       sbc(es, "Wod%d" % k, [128, 8, 128], BF16), sbc(es, "Woc%d" % k, [128, 4, 128], BF16)] for k in range(2)]
                Wout = sbc(es, "Wout", [128, 4, D], BF16)
                SA = sbc(es, "SA", [128, 512]); SBt = sbc(es, "SBt", [128, 512])

                def load_w3(c):
                    w = W3[c % 2]
                    wload(w[0][:], w_in[l, :, C_GA + c * 128:C_GA + (c + 1) * 128])
                    wload(w[1][:], w_in[l, :, C_GB + c * 128:C_GB + (c + 1) * 128])
                    wload(w[2][:], w_o_delta[l, :, c * 128:(c + 1) * 128])
                    wload(w[3][:], w_o_conf[l, :, c * 128:(c + 1) * 128])

                load_w3(0)
                for grp in range(2):
                    wload(Wout[:], w_out[l, grp * 512:(grp + 1) * 512, :])
                    for cc in range(4):
                        c = grp * 4 + cc
                        if c + 1 < 8:
                            load_w3(c + 1)
                        WgA, WgB, Wod, Woc = W3[c % 2]
                        for (g0, gw) in GROUPS:
                            b1 = pb(); b2 = pb(); b3 = pb(); b4 = pb()
                            proj(b1, WgA, 8, HT, g0, gw)
                            proj(b2, WgB, 8, HT, g0, gw)
                            proj(b3, Wod, 8, OG, g0, gw)
                            proj(b4, Woc, 4, CB, g0, gw)
                            s.act(SA[:, 0:gw], b1[:, 0:gw], AF.Sigmoid)
                            s.act(SBt[:, 0:gw], b2[:, 0:gw], AF.Sigmoid)
                            s.tt(SA[:, 0:gw], b3[:, 0:gw], SA[:, 0:gw], ALU.mult)
                            s.tt(SBt[:, 0:gw], b4[:, 0:gw], SBt[:, 0:gw], ALU.mult)
                            s.tt(MG[:, cc, g0:g0 + gw], SA[:, 0:gw], SBt[:, 0:gw], ALU.add, eng="pool")
                    for ti in range(17):
                        t0, n = TILES[ti]
                        xa, _ = Xt(ti)
                        for hf in range(2):
                            bank = pb()
                            for cc in range(4):
                                s.mm(bank[0:n, :], MG[:, cc, t0:t0 + n], Wout[:, cc, hf * 512:(hf + 1) * 512], start=(cc == 0), stop=(cc == 3))
                            s.tt(xa[0:n, hf * 512:(hf + 1) * 512], xa[0:n, hf * 512:(hf + 1) * 512], bank[0:n, :], ALU.add)
                barrier()
            ckpt(8)
            esL.close()

            norm_to_HT(norm_ffn_w[l])
            with ExitStack() as es:
                FH = sbc(es, "FH", [128, 44, NS, 2], BF16)
                FT = sbc(es, "FT", [128, 44, 18])
                fcw = sbc(es, "fcw", [128, 44, 3]); fcb = sbc(es, "fcb", [128, 44])
                for j in range(3):
                    s.dma(fcw[:, :, j], ffn_conv_w[l, j].rearrange("(c p) -> p c", p=128))
                fm_vec(fcb[:], ffn_conv_b[l])
                with ExitStack() as es2:
                    ST3 = sbc(es2, "ST3", [32, 2 * DFF])
                    s.dma(ST3[:, :], sffn[l].rearrange("s j c -> (s j) c"))
                    for c16 in range(3):
                        bank = pb()
                        ncs = min(16, 44 - c16 * 16)
                        for cc in range(ncs):
                            c = c16 * 16 + cc
                            s.tr(bank[:, cc * 32:(cc + 1) * 32], ST3[0:32, c * 128:(c + 1) * 128], idf[0:32, 0:32])
                        s.cp(FH[:, c16 * 16:c16 * 16 + ncs, :, :].rearrange("p c s j -> p (c s j)"), bank[:, 0:ncs * 32], eng="act")
                    barrier()
                with ExitStack() as es2:
                    A = sbc(es2, "A", [128, 11, TT], BF16)
                    UPd = [[sbc(es2, "UP%d_%d" % (k, i), [128, 2 + TT], BF16) for i in range(2)] for k in range(2)]
                    Wud = [sbc(es2, "Wu%d" % k, [128, 8, 2, 128], BF16) for k in range(2)]
                    DGfd = [sbc(es2, "DGf%d" % k, [128, 2, 3, 128], BF16) for k in range(2)]
                    Wdh = [sbc(es2, "Wd%d" % k, [128, 11, 512], BF16) for k in range(2)]
                    SGt = [sbc(es2, "SGt", [128, 512])] * 2
                    for k in range(2):
                        for i in range(2):
                            s.memset(UPd[k][i][:, 0:2], 0.0)
                    bkP = [0]; bkC = [0]

                    def pbP():
                        bkP[0] += 1
                        return banks[bkP[0] % 4]

                    def pbC():
                        bkC[0] += 1
                        return banks[4 + bkC[0] % 4]

                    def load_wu(c):
                        for i, ch in enumerate((c, 22 + c)):
                            wload(Wud[c % 2][:, :, i, :], w_up[l, :, ch * 128:(ch + 1) * 128])

                    def genP(c):
                        par = c % 2
                        chs = (c, 22 + c)
                        UP = UPd[par]; DGf = DGfd[par]
                        Wu = Wud[par]
                        if c + 1 < 22:
                            load_wu(c + 1)
                        for i in range(2):
                            for j in range(3):
                                s.ts(DGf[:, i, j, :], idb[:], fcw[:, chs[i], j:j + 1], None, op0=ALU.mult)
                        yield
                        for (g0, gw) in GROUPS:
                            for i in range(2):
                                bank = pbP()
                                proj(bank, Wu[:, :, i, :], 8, HT, g0, gw)
                                s.cp(UP[i][:, 2 + g0:2 + g0 + gw], bank[:, 0:gw], eng=("act" if i == 0 else "dve"))
                                if g0 == 1536:
                                    s.cp(FT[:, chs[i], 0:2], bank[:, 510:512])
                                if g0 == T:
                                    s.cp(FT[:, chs[i], 2:18], bank[:, 0:NS])
                                yield

                    def genC(c):
                        par = c % 2
                        cc = c % 11
                        chs = (c, 22 + c)
                        UP = UPd[par]; DGf = DGfd[par]
                        for gi, (g0, gw) in enumerate(GROUPS):
                            bb = [pbC(), pbC()]
                            for i in range(2):
                                for j in range(3):
                                    if g0 < T:
                                        rhs = UP[i][:, g0 + j:g0 + j + gw]
                                    else:
                                        rhs = FH[:, chs[i], :, j] if j < 2 else UP[i][:, 2 + T:2 + TT]
                                    s.mm(bb[i][:, 0:gw], DGf[:, i, j, :], rhs, start=(j == 0), stop=(j == 2))
                            sg = SGt[gi % 2]
                            s.act(sg[:, 0:gw], bb[0][:, 0:gw], AF.Silu, bias=fcb[:, chs[0]:chs[0] + 1])
                            s.stt(A[:, cc, g0:g0 + gw], bb[1][:, 0:gw], fcb[:, chs[1]:chs[1] + 1], sg[:, 0:gw], op0=ALU.add, op1=ALU.mult)
                            yield

                    def genD(grp):
                        for hf in range(2):
                            Wd = Wdh[hf]
                            for ti in range(17):
                                t0, n = TILES[ti]
                                xa, _ = Xt(ti)
                                bank = pbC()
                                for cc in range(11):
                                    s.mm(bank[0:n, :], A[:, cc, t0:t0 + n], Wd[:, cc, :], start=(cc == 0), stop=(cc == 10))
                                s.tt(xa[0:n, hf * 512:(hf + 1) * 512], xa[0:n, hf * 512:(hf + 1) * 512], bank[0:n, :], ALU.add)
                                yield

                    def step2(gens, weights):
                        live = [[g_, w_] for g_, w_ in zip(gens, weights) if g_ is not None]
                        while live:
                            for item in list(live):
                                for _ in range(item[1]):
                                    try:
                                        next(item[0])
                                    except StopIteration:
                                        live.remove(item)
                                        break

                    load_wu(0)
                    step2([genP(0)], [1])
                    for c in range(22):
                        if c % 11 == 0:
                            for hf in range(2):
                                wload(Wdh[hf][:], w_down[l, (c // 11) * 1408:(c // 11 + 1) * 1408, hf * 512:(hf + 1) * 512])
                        step2([genP(c + 1) if c + 1 < 22 else None, genC(c)], [2, 1])
                        if c % 11 == 10:
                            step2([genD(c // 11)], [1])
                    barrier()
                with ExitStack() as es2:
                    TF = sbc(es2, "TF", [18, 2 * DFF])
                    for c4 in range(11):
                        bank = pb()
                        for cc in range(4):
                            c = c4 * 4 + cc
                            s.tr(bank[0:18, cc * 128:(cc + 1) * 128], FT[:, c, :], idf[:])
                        s.cp(TF[:, c4 * 512:(c4 + 1) * 512], bank[0:18, :])
                    s.dma(nf_p[l], TF[0:2, :])
                    s.dma(nf_s[l, :, 1, :], TF[2:18, :])
                    barrier()

        except _Stop:
            pass
        with ExitStack() as es:
            junk = sbc(es, "junkf", [128, D], BF16)
            wbc = sbc(es, "wbc", [128, D])
            Y = [sbc(es, "Y%d" % i, [128, D]) for i in range(2)]
            s.dma(wbc[:], norm_final_w.rearrange("(o d) -> o d", o=1).broadcast_to([128, D]))
            for ti in range(17):
                xa, n = Xt(ti)
                s.act(junk[0:n, :], xa[0:n], AF.Square, accum_out=ss[0:n, ti:ti + 1])
            s.ts(rstd[:], ss[:], 1.0 / D, EPS, op0=ALU.mult, op1=ALU.add)
            s.act(rstd[:], rstd[:], AF.Sqrt)
            s.recip(rstd[:], rstd[:])
            for ti in range(17):
                xa, n = Xt(ti)
                t0 = TILES[ti][0]
                y = Y[ti % 2]
                s.stt(y[0:n, :], xa[0:n], rstd[0:n, ti:ti + 1], wbc[0:n, :], op0=ALU.mult, op1=ALU.mult)
                if ti < 16:
                    s.dma(yp[t0:t0 + 128, :], y[:, :])
                else:
                    s.dma(ys, y[0:NS, :])
            barrier()
        s.emit()
    return nc, s


_CACHE = {}

IN_NAMES = ["norm_mix_w", "w_in", "conv_qkv_w", "a_log", "dt_bias", "delta_norm_w", "w_o_delta", "conf_conv_w",
            "conf_conv_b", "conf_ln_w", "conf_ln_b", "w_o_conf", "w_out", "norm_ffn_w", "w_up", "ffn_conv_w",
            "ffn_conv_b", "w_down", "norm_final_w"]


def kernel(x_prompt, x_sample, state_delta, state_qkv_conv, state_conf_conv, state_ffn_conv, _cores=None, _upto=99, **w):
    cores = list(range(8)) if _cores is None else list(_cores)
    if _upto not in _CACHE:
        _CACHE[_upto] = build_program(upto=_upto)[0]
    nc = _CACHE[_upto]
    f = lambda a: np.ascontiguousarray(np.asarray(a, dtype=np.float32))
    wts = {k: f(w[k]) for k in IN_NAMES}
    in_maps = []
    for c in cores:
        m = dict(wts)
        m["xp"] = f(x_prompt[c]); m["xs"] = f(x_sample[16 * c:16 * c + 16, 0])
        m["sdelta"] = f(state_delta[:, 16 * c:16 * c + 16]); m["sqkv"] = f(state_qkv_conv[:, 16 * c:16 * c + 16])
        m["sconf"] = f(state_conf_conv[:, 16 * c:16 * c + 16]); m["sffn"] = f(state_ffn_conv[:, 16 * c:16 * c + 16])
        in_maps.append(m)
    res = run_bass_kernel_spmd(nc, in_maps, core_ids=list(range(len(cores))))
    R = res.results
    nco = len(cores)
    y_prompt = np.stack([R[i]["yp"] for i in range(nco)], 0)
    y_sample = np.concatenate([R[i]["ys"] for i in range(nco)], 0)[:, None, :]
    pst = lambda k: np.stack([R[i][k] for i in range(nco)], 1)
    sst = lambda k: np.concatenate([R[i][k] for i in range(nco)], 1)
    return (y_prompt.astype(np.float32), y_sample.astype(np.float32), pst("nd_p"), pst("nq_p"), pst("nc_p"), pst("nf_p"),
            sst("nd_s"), sst("nq_s"), sst("nc_s"), sst("nf_s"))
```

```python
import numpy as np
import concourse.bass as bass
import concourse.mybir as mybir

F32 = mybir.dt.float32
BF16 = mybir.dt.bfloat16
F32R = mybir.dt.float32r
AF = mybir.ActivationFunctionType
ALU = mybir.AluOpType
AX = mybir.AxisListType


class _Buf:
    __slots__ = ("writers", "readers")

    def __init__(self):
        self.writers = []
        self.readers = []


class _Op:
    __slots__ = ("eng", "fn", "deps", "dma", "marked", "token", "idx")


class Sched:
    INORDER = ("pe", "dve", "act", "pool")

    def __init__(self, nc, n_dma_sems=40):
        self.nc = nc
        self.ops = []
        self.bufs = {}
        self.engs = {"pe": nc.tensor, "dve": nc.vector, "act": nc.scalar,
                     "pool": nc.gpsimd, "sp": nc.sync}
        self.n_dma_sems = n_dma_sems

    def _buf(self, ap):
        name = ap.tensor.name if hasattr(ap, "tensor") else ap.name
        b = self.bufs.get(name)
        if b is None:
            b = self.bufs[name] = _Buf()
        return b

    def add(self, eng, fn, reads=(), writes=(), dma=False):
        op = _Op()
        op.eng, op.fn, op.dma, op.marked, op.token = eng, fn, dma, False, None
        op.idx = len(self.ops)
        deps = set()
        rb = []
        for a in reads:
            if a is None or isinstance(a, (int, float)):
                continue
            b = self._buf(a)
            if b not in rb:
                rb.append(b)
        wb = []
        for a in writes:
            if a is None:
                continue
            b = self._buf(a)
            if b not in wb:
                wb.append(b)
        for a in reads:
            if a is None or isinstance(a, (int, float)):
                continue
            nm = a.tensor.name if hasattr(a, "tensor") else a.name
            if nm.startswith("bank"):
                deps.update(self._buf(a).readers)
        for b in rb:
            deps.update(b.writers)
        for b in wb:
            deps.update(b.readers)
            deps.update(b.writers)
        for b in rb:
            if b not in wb:
                b.readers.append(op.idx)
        for b in wb:
            if b.readers or len(b.writers) > 64:
                b.writers = [op.idx]
                b.readers = []
            else:
                b.writers.append(op.idx)
        red = {}
        for j in deps:
            o = self.ops[j]
            if o.dma:
                red[("dma", j)] = j
            else:
                k = o.eng
                if k not in red or red[k] < j:
                    red[k] = j
        op.deps = sorted(red.values())
        self.ops.append(op)
        return op

    def emit(self, final_wait_eng="sp"):
        nc = self.nc
        ops = self.ops
        for op in ops:
            for j in op.deps:
                o = ops[j]
                if (not o.dma) and o.eng == "pe" and op.eng == "pe" and not op.dma:
                    continue
                o.marked = True
        sems = {k: nc.alloc_semaphore("sq_" + k) for k in self.INORDER}
        dsems = [nc.alloc_semaphore("sd_%d" % i) for i in range(self.n_dma_sems)]
        dcount = [0] * self.n_dma_sems
        counts = {k: 0 for k in self.INORDER}
        seen = {k: {} for k in self.engs}
        ndma = 0
        for op in ops:
            e = self.engs[op.eng]
            sn = seen[op.eng]
            waits = []
            for j in op.deps:
                o = ops[j]
                if (not o.dma) and o.eng == "pe" and op.eng == "pe" and not op.dma:
                    continue
                waits.append(o.token)
            slot = None
            if op.dma:
                slot = ndma % self.n_dma_sems
                ndma += 1
                if dcount[slot] > 0:
                    waits.append((dsems[slot], dcount[slot] * 16, "d%d" % slot))
            wmax = {}
            for (s, v, key) in waits:
                if key not in wmax or wmax[key][1] < v:
                    wmax[key] = (s, v)
            for key, (s, v) in wmax.items():
                if sn.get(key, 0) >= v:
                    continue
                sn[key] = v
                e.wait_ge(s, v)
            ins = op.fn(e)
            if op.dma:
                dcount[slot] += 1
                ins.then_inc(dsems[slot], 16)
                op.token = (dsems[slot], dcount[slot] * 16, "d%d" % slot)
            elif op.marked:
                counts[op.eng] += 1
                ins.then_inc(sems[op.eng], 1)
                op.token = (sems[op.eng], counts[op.eng], op.eng)
            else:
                op.token = (sems[op.eng], counts[op.eng] + 0, op.eng)
        e = self.engs[final_wait_eng]
        for slot in range(self.n_dma_sems):
            if dcount[slot] > 0:
                e.wait_ge(dsems[slot], dcount[slot] * 16)
        for k in self.INORDER:
            if counts[k] > 0:
                e.wait_ge(sems[k], counts[k])
        self.stats = dict(n_ops=len(ops), counts=counts, ndma=ndma)

    def mm(self, out, lhsT, rhs, start=True, stop=True):
        return self.add("pe", lambda e: e.matmul(out, lhsT, rhs, start=start, stop=stop),
                        reads=[lhsT, rhs], writes=[out])

    def tr(self, out, in_, ident):
        return self.add("pe", lambda e: e.transpose(out, in_, ident), reads=[in_, ident], writes=[out])

    def act(self, out, in_, func, bias=0.0, scale=1.0, accum_out=None):
        kw = {}
        if accum_out is not None:
            kw["accum_out"] = accum_out
        return self.add("act", lambda e: e.activation(out, in_, func, bias=bias, scale=scale, **kw),
                        reads=[in_, bias if not isinstance(bias, float) else None,
                               scale if not isinstance(scale, float) else None],
                        writes=[out, accum_out])

    def ts(self, out, in0, s1, s2=None, op0=ALU.mult, op1=None, eng="dve", accum_out=None):
        kw = {}
        if op1 is not None:
            kw["op1"] = op1
        if accum_out is not None:
            kw["accum_out"] = accum_out
        return self.add(eng, lambda e: e.tensor_scalar(out, in0, s1, s2, op0=op0, **kw),
                        reads=[in0, s1 if not isinstance(s1, (int, float)) else None,
                               s2 if not isinstance(s2, (int, float)) else None],
                        writes=[out, accum_out])

    def tt(self, out, in0, in1, op, eng="dve"):
        return self.add(eng, lambda e: e.tensor_tensor(out, in0, in1, op), reads=[in0, in1], writes=[out])

    def stt(self, out, in0, scalar, in1, op0, op1):
        return self.add("dve", lambda e: e.scalar_tensor_tensor(out, in0, scalar, in1, op0=op0, op1=op1),
                        reads=[in0, in1, scalar if not isinstance(scalar, (int, float)) else None],
                        writes=[out])

    def cp(self, out, in_, eng="dve"):
        if eng == "act":
            return self.add("act", lambda e: e.copy(out, in_), reads=[in_], writes=[out])
        return self.add(eng, lambda e: e.tensor_copy(out, in_), reads=[in_], writes=[out])

    def recip(self, out, in_):
        return self.add("dve", lambda e: e.reciprocal(out, in_), reads=[in_], writes=[out])

    def memset(self, ap, val, eng="dve"):
        return self.add(eng, lambda e: e.memset(ap, val), writes=[ap])

    def dma(self, out, in_, eng="sp", **kw):
        return self.add(eng, lambda e: e.dma_start(out, in_, **kw), reads=[in_], writes=[out], dma=True)

import numpy as np
from concourse.bass_utils import run_bass_kernel_spmd

from contextlib import ExitStack
import os

D = 1024
T = 2048
NS = 16
TT = T + NS
NH = 8
QKV = 3072
IN_COLS = 7184
DFF = 2816
EPS = 1e-6
GROUPS = [(0, 512), (512, 512), (1024, 512), (1536, 512), (2048, 16)]
TILES = [(t * 128, 128) for t in range(16)] + [(2048, 16)]
C_Z = 3072
C_BA = 4096
C_GLU = 4112
C_GA = 5136
C_GB = 6160
NEG = -30000.0


class _Stop(Exception):
    pass


def build_program(depth=2, upto=99):
    nc = bass.Bass("TRN2", target_bir_lowering=False)
    s = Sched(nc)
    uid = [0]

    def din(name, shape):
        return nc.dram_tensor(name, list(shape), F32, kind="ExternalInput").ap()

    def dout(name, shape):
        return nc.dram_tensor(name, list(shape), F32, kind="ExternalOutput").ap()

    xp = din("xp", [T, D]); xs = din("xs", [NS, D])
    sdelta = din("sdelta", [2, NS, NH, 128, 128])
    sqkv = din("sqkv", [2, NS, 3, QKV]); sconf = din("sconf", [2, NS, 30, 512]); sffn = din("sffn", [2, NS, 2, 2 * DFF])
    norm_mix_w = din("norm_mix_w", [2, D]); w_in = din("w_in", [2, D, IN_COLS]); conv_qkv_w = din("conv_qkv_w", [2, 4, QKV])
    a_log = din("a_log", [2, NH]); dt_bias = din("dt_bias", [2, NH]); delta_norm_w = din("delta_norm_w", [2, 128])
    w_o_delta = din("w_o_delta", [2, D, D]); conf_conv_w = din("conf_conv_w", [2, 31, 512]); conf_conv_b = din("conf_conv_b", [2, 512])
    conf_ln_w = din("conf_ln_w", [2, 512]); conf_ln_b = din("conf_ln_b", [2, 512]); w_o_conf = din("w_o_conf", [2, 512, D])
    w_out = din("w_out", [2, D, D]); norm_ffn_w = din("norm_ffn_w", [2, D]); w_up = din("w_up", [2, D, 2 * DFF])
    ffn_conv_w = din("ffn_conv_w", [2, 3, 2 * DFF]); ffn_conv_b = din("ffn_conv_b", [2, 2 * DFF]); w_down = din("w_down", [2, DFF, D])
    norm_final_w = din("norm_final_w", [D])

    yp = dout("yp", [T, D]); ys = dout("ys", [NS, D])
    nd_p = dout("nd_p", [2, NH, 128, 128]); nq_p = dout("nq_p", [2, 3, QKV]); nc_p = dout("nc_p", [2, 30, 512]); nf_p = dout("nf_p", [2, 2, 2 * DFF])
    nd_s = dout("nd_s", [2, NS, NH, 128, 128]); nq_s = dout("nq_s", [2, NS, 3, QKV]); nc_s = dout("nc_s", [2, NS, 30, 512]); nf_s = dout("nf_s", [2, NS, 2, 2 * DFF])

    def sb(name, shape, dt=F32):
        uid[0] += 1
        return nc.alloc_sbuf_tensor("%s_%d" % (name, uid[0]), list(shape), dt)

    def sbc(es, name, shape, dt=F32):
        uid[0] += 1
        return es.enter_context(nc.sbuf_tensor("%s_%d" % (name, uid[0]), list(shape), dt))

    banks = [nc.alloc_psum_tensor("bank%d" % i, [128, 512], F32) for i in range(8)]
    bk = [0]

    def pb():
        b = banks[bk[0] % 8]
        bk[0] += 1
        return b

    def bfv(bank):
        return bank[:].bitcast(BF16)

    idf = sb("idf", [128, 128]); idb = sb("idb", [128, 128], BF16)
    ones_f = sb("ones_f", [128, 128]); ones_b = sb("ones_b", [128, 128], BF16)
    triU = sb("triU", [128, 128]); negmask = sb("negmask", [128, 128]); strict01 = sb("strict01", [128, 128])
    X = sb("X", [128, 16, D]); XS = sb("XS", [16, D])
    HT = sb("HT", [128, 8, TT], BF16)
    ss = sb("ss", [128, 17]); rstd = sb("rstd", [128, 17])
    normw = sb("normw", [128, 8])

    def Xt(ti):
        t0, n = TILES[ti]
        return (X[:, ti, :] if ti < 16 else XS[:, :]), n

    bar1 = {k: sb("bar1_" + k, [128, 1]) for k in ("pe", "dve", "act", "pool", "sp")}
    bar2 = {k: sb("bar2_" + k, [128, 1]) for k in ("pe", "dve", "act", "pool", "sp")}

    def barrier():
        import os
        if os.environ.get("KNOBAR"):
            return
        which = os.environ.get("KBAR", "pe,dve,act,pool,sp").split(",")
        for (br, rd) in ((bar1, []), (bar2, [bar1[k][:] for k in bar1 if k in which])):
            b = pb()
            if "pe" in which:
                s.add("pe", lambda e, b=b: e.matmul(b[0:1, 0:2], ones_f[:, 0:1], ones_f[:, 0:2], start=True, stop=True),
                      reads=rd + [ones_f[:]], writes=[b[:], br["pe"][:]])
            if "dve" in which:
                s.add("dve", lambda e, br=br: e.memset(br["dve"][:], 0.0), reads=rd, writes=[br["dve"][:]])
            if "act" in which:
                s.add("act", lambda e, br=br: e.activation(br["act"][:], ones_f[:, 0:1], AF.Copy), reads=rd + [ones_f[:]], writes=[br["act"][:]])
            if "pool" in which:
                s.add("pool", lambda e, br=br: e.memset(br["pool"][:], 0.0), reads=rd, writes=[br["pool"][:]])
            if "sp" in which:
                op = s.add("sp", lambda e, br=br: e.dma_start(br["sp"][0:1, 0:1], ones_f[0:1, 0:1]), reads=rd + [ones_f[:]],
                           writes=[br["sp"][:]], dma=True)
                if not rd and not os.environ.get("KNOALLD"):
                    alld = [o.idx for o in s.ops if o.dma and o.idx > s.last_bar and o.idx != op.idx]
                    op.deps = sorted(set(op.deps) | set(alld))
                    s.last_bar = op.idx

    s.last_bar = -1

    def ckpt(k):
        if upto == k:
            barrier()
            raise _Stop()

    def wload(dst, src):
        s.dma(dst, src.rearrange("(k p) n -> p k n", p=128), eng="pool")

    def fm_vec(dst, src):
        s.dma(dst, src.rearrange("(k p) -> p k", p=128))

    es0 = ExitStack()
    with nc.allow_non_contiguous_dma(reason="small strided parameter loads"), nc.allow_low_precision(reason="bf16 matmuls"), es0:
        s.memset(idf[:], 0.0, eng="pool")
        s.add("pool", lambda e: e.affine_select(idf[:], idf[:], pattern=[[-1, 128]], compare_op=ALU.not_equal,
                                                  fill=1.0, base=0, channel_multiplier=1), reads=[idf[:]], writes=[idf[:]])
        s.cp(idb[:], idf[:])
        s.memset(ones_f[:], 1.0); s.memset(ones_b[:], 1.0)
        s.memset(triU[:], 1.0, eng="pool")
        s.add("pool", lambda e: e.affine_select(triU[:], triU[:], pattern=[[1, 128]], compare_op=ALU.is_ge,
                                                  fill=0.0, base=0, channel_multiplier=-1), reads=[triU[:]], writes=[triU[:]])
        s.memset(negmask[:], 0.0, eng="pool")
        s.add("pool", lambda e: e.affine_select(negmask[:], negmask[:], pattern=[[1, 128]], compare_op=ALU.is_ge,
                                                  fill=NEG, base=0, channel_multiplier=-1), reads=[negmask[:]], writes=[negmask[:]])
        s.memset(strict01[:], 1.0, eng="pool")
        s.add("pool", lambda e: e.affine_select(strict01[:], strict01[:], pattern=[[1, 128]], compare_op=ALU.is_ge,
                                                  fill=0.0, base=-1, channel_multiplier=-1), reads=[strict01[:]], writes=[strict01[:]])
        s.memset(ss[:], 1.0)
        for q4 in range(4):
            s.dma(X[:, q4 * 4:(q4 + 1) * 4, :], xp[q4 * 512:(q4 + 1) * 512, :].rearrange("(t p) d -> p t d", p=128))
        s.dma(XS[:, :], xs)

        def norm_to_HT(wrow):
            with ExitStack() as es:
                junk = sbc(es, "junk", [128, D], BF16)
                xb = sbc(es, "xb", [128, D], BF16)
                fm_vec(normw[:], wrow)
                for ti in range(17):
                    xa, n = Xt(ti)
                    s.act(junk[0:n, :], xa[0:n], AF.Square, accum_out=ss[0:n, ti:ti + 1])
                s.ts(rstd[:], ss[:], 1.0 / D, EPS, op0=ALU.mult, op1=ALU.add)
                s.act(rstd[:], rstd[:], AF.Sqrt)
                s.recip(rstd[:], rstd[:])
                xbs = [xb, junk]
                xa0, n0 = Xt(0)
                s.act(xbs[0][0:n0, :], xa0[0:n0], AF.Copy, scale=rstd[0:n0, 0:1])
                for ti in range(17):
                    xa, n = Xt(ti)
                    t0 = TILES[ti][0]
                    if ti + 1 < 17:
                        xa1, n1 = Xt(ti + 1)
                        s.act(xbs[(ti + 1) % 2][0:n1, :], xa1[0:n1], AF.Copy, scale=rstd[0:n1, ti + 1:ti + 2])
                    bank = pb()
                    pv = bfv(bank).rearrange("p (k t) -> p k t", k=8)
                    for k in range(8):
                        s.tr(pv[:, k, 0:n], xbs[ti % 2][0:n, k * 128:(k + 1) * 128], idb[0:n, 0:n])
                    s.tt(HT[:, :, t0:t0 + n], pv[:, :, 0:n], normw[:].unsqueeze(2).broadcast_to([128, 8, n]), ALU.mult)
                barrier()

        def proj(bank, W, kn, src, g0, gw):
            for k in range(kn):
                s.mm(bank[:, 0:gw], W[:, k, :], src[:, k, g0:g0 + gw], start=(k == 0), stop=(k == kn - 1))

        try:
          for l in range(depth):
            ckpt(1)
            norm_to_HT(norm_mix_w[l])
            ckpt(2)
            esL = ExitStack()
            SQT = sbc(esL, "SQT", [128, 24, NS, 3], BF16)
            TAILQ = sbc(esL, "TAILQ", [128, 24, 19])
            BETA = sbc(esL, "BETA", [128, 17, 8]); GC = sbc(esL, "GC", [128, 17, 8]); EGC = sbc(esL, "EGC", [128, 17, 8])
            KSC2 = sbc(esL, "KSC2", [128, 16, 8]); GLAST = sbc(esL, "GLAST", [128, 16, 8])
            BETAS = sbc(esL, "BETAS", [128, NS, 8]); EGS = sbc(esL, "EGS", [128, NS, 8])
            OG = sbc(esL, "OG", [128, 8, TT], BF16)
            cw = sbc(esL, "cw", [128, 24, 4]); dnw = sbc(esL, "dnw", [128, 1])
            for j in range(4):
                s.dma(cw[:, :, j], conv_qkv_w[l, j].rearrange("(c p) -> p c", p=128))
            s.dma(dnw[:, 0:1], delta_norm_w[l].rearrange("(p o) -> p o", o=1))
            s.dma(nq_s[l, :, 0:2, :], sqkv[l, :, 1:3, :])
            s.dma(nc_s[l, :, 0:29, :], sconf[l, :, 1:30, :])
            s.dma(nf_s[l, :, 0:1, :], sffn[l, :, 1:2, :])
            with ExitStack() as es:
                ckpt(21)
                ST = sbc(es, "ST", [48, QKV])
                s.dma(ST[:, :], sqkv[l].rearrange("s j c -> (s j) c"))
                for c4 in range(6):
                    bank = pb()
                    for cc in range(4):
                        c = c4 * 4 + cc
                        s.tr(bank[:, cc * 48:(cc + 1) * 48], ST[0:48, c * 128:(c + 1) * 128], idf[0:48, 0:48])
                    s.cp(SQT[:, c4 * 4:(c4 + 1) * 4, :, :].rearrange("p c s j -> p (c s j)"), bank[:, 0:192], eng="act")
                ckpt(22)
                Wba = sbc(es, "Wba", [128, 8, 16], BF16)
                wload(Wba[:], w_in[l, :, C_BA:C_BA + 16])
                BA = sbc(es, "BA", [128, 17, 16]); T1 = sbc(es, "T1", [128, 17, 8]); G = sbc(es, "G", [128, 17, 8])
                GCL = sbc(es, "GCL", [128, 16, 8])
                dtb = sbc(es, "dtb", [128, 8]); negA = sbc(es, "negA", [128, 8])
                s.dma(dtb[:], dt_bias[l:l + 1, :].broadcast_to([128, 8]))
                s.dma(negA[:], a_log[l:l + 1, :].broadcast_to([128, 8]))
                s.act(negA[:], negA[:], AF.Exp)
                s.ts(negA[:], negA[:], -1.0, None, op0=ALU.mult)
                s.memset(BA[:], 0.0)
                bank = pb()
                for ti in range(17):
                    t0, n = TILES[ti]
                    for k in range(8):
                        s.mm(bank[0:n, ti * 16:(ti + 1) * 16], HT[:, k, t0:t0 + n], Wba[:, k, :], start=(k == 0), stop=(k == 7))
                s.cp(BA[:, 0:16, :].rearrange("p t c -> p (t c)"), bank[:, 0:256])
                s.cp(BA[0:16, 16, :], bank[0:16, 256:272])
                ckpt(23)
                s.act(BETA[:], BA[:, :, 0:8], AF.Sigmoid)
                s.tt(T1[:], BA[:, :, 8:16], dtb[:].unsqueeze(1).broadcast_to([128, 17, 8]), ALU.add)
                UE = sbc(es, "UE", [128, 17, 8]); UC = sbc(es, "UC", [128, 17, 8]); RP = sbc(es, "RP", [128, 17, 8]); MK = sbc(es, "MK", [128, 17, 8])
                s.act(UE[:], T1[:], AF.Exp)
                s.act(T1[:], UE[:], AF.Ln, bias=1.0)
                s.ts(UC[:], UE[:], 0.5, None, op0=ALU.min)
                s.ts(RP[:], UC[:], -1.0 / 8, None, op0=ALU.mult)
                for cf in (1.0 / 7, -1.0 / 6, 1.0 / 5, -1.0 / 4, 1.0 / 3, -1.0 / 2, 1.0):
                    s.stt(RP[:], RP[:], cf, UC[:], op0=ALU.add, op1=ALU.mult)
                s.ts(MK[:], UE[:], 0.35, None, op0=ALU.is_lt)
                s.tt(RP[:], RP[:], T1[:], ALU.subtract)
                s.tt(RP[:], RP[:], MK[:], ALU.mult)
                s.tt(T1[:], T1[:], RP[:], ALU.add)
                s.tt(G[:], T1[:], negA[:].unsqueeze(1).broadcast_to([128, 17, 8]), ALU.mult)
                ckpt(25)
                bank = pb()
                Gp = G[:, 0:16, :].rearrange("p t h -> p (t h)")
                bank2 = pb()
                s.mm(bank[:, 0:128], triU[:], Gp)
                s.cp(GC[:, 0:16, :].rearrange("p t h -> p (t h)"), bank[:, 0:128])
                s.mm(bank2[:, 0:128], ones_f[:], Gp)
                s.cp(GCL[:].rearrange("p t h -> p (t h)"), bank2[:, 0:128])
                s.cp(GC[0:16, 16, :], G[0:16, 16, :])
                ckpt(26)
                s.act(EGC[:], GC[:], AF.Exp)
                s.tt(KSC2[:], GCL[:], GC[:, 0:16, :], ALU.subtract)
                s.act(KSC2[:], KSC2[:], AF.Exp)
                s.act(GLAST[:], GCL[:], AF.Exp)
                ckpt(24)
                BD = sbc(es, "BD", [16, NS, 8])
                for (src, dst) in ((BETA, BETAS), (EGC, EGS)):
                    s.tt(BD[:], src[0:16, 16, :].unsqueeze(1).broadcast_to([16, NS, 8]),
                         idf[0:16, 0:16].unsqueeze(2).broadcast_to([16, NS, 8]), ALU.mult)
                    bank = pb()
                    s.mm(bank[:, 0:128], ones_f[0:16, :], BD[:].rearrange("p s h -> p (s h)"))
                    s.cp(dst[:].rearrange("p s h -> p (s h)"), bank[:, 0:128])
                barrier()

            ckpt(3)
            with ExitStack() as es:
                Wq = [sbc(es, "Wq%d" % i, [128, 8, 128], BF16) for i in range(4)]
                PRE = [sbc(es, "PRE%d" % i, [128, 515], BF16) for i in range(3)]
                PREs = sbc(es, "PREs", [128, 3, NS], BF16)
                DGq = sbc(es, "DGq", [128, 3, 4, 128], BF16)
                QKd = [[sbc(es, "QKg%d_%d" % (i, k), [128, 512], BF16) for i in range(3)] for k in range(2)]
                SQB = sbc(es, "SQB", [128, 512], BF16); R1 = sbc(es, "R1", [128, 512])
                SQB2 = sbc(es, "SQB2", [128, 512], BF16); R1b = sbc(es, "R1b", [128, 512]); TMPn = sbc(es, "TMPn", [128, 512], BF16)
                SQB3 = sbc(es, "SQB3", [128, NS], BF16); R1c = sbc(es, "R1c", [128, NS]); TMPs = sbc(es, "TMPs", [128, NS]); OTs = sbc(es, "OTs", [128, NS])
                ZSs = sbc(es, "ZSs", [128, NS], BF16)
                DIAG4 = sbc(es, "DIAG4", [128, 4, 128]); DTm4 = sbc(es, "DTm4", [128, 4, 128])
                DECI4 = sbc(es, "DECI4", [128, 4, 128], BF16); EGB4 = sbc(es, "EGB4", [128, 4, 128], BF16)
                DECS4 = EGB4
                MLp = [sbc(es, "MLp%d" % i, [128, 2, 2, 128]) for i in range(2)]
                PPp = [sbc(es, "PPp%d" % i, [128, 2, 128]) for i in range(2)]
                KGC = [sbc(es, "KGC%d" % i, [128, 4, 128], BF16) for i in range(2)]
                KG = [sbc(es, "KG%d" % i, [128, 4, 128], BF16) for i in range(2)]
                VT = [sbc(es, "VT%d" % i, [128, 4, 128], BF16) for i in range(2)]
                QG = [sbc(es, "QG%d" % i, [128, 4, 128], BF16) for i in range(2)]
                QKT = [sbc(es, "QKT%d" % i, [128, 4, 128], BF16) for i in range(2)]
                Pb = [sbc(es, "Pb%d" % i, [128, 4, 128], BF16) for i in range(2)]
                NWT = [sbc(es, "NWT%d" % i, [128, 4, 128], BF16) for i in range(2)]
                ZSg = [sbc(es, "ZSg%d" % i, [128, 512], BF16) for i in range(3)]
                VN = sbc(es, "VN", [128, 128], BF16)
                S = sbc(es, "S", [128, 128]); Sb = sbc(es, "Sb", [128, 128], BF16)
                OTg = sbc(es, "OTg", [128, 512], BF16)
                QKs = sbc(es, "QKs", [128, 3, NS])
                SDd = [sbc(es, "SD%d" % k, [128, 1, 128]) for k in range(2)]
                SDnd = [sbc(es, "SDn%d" % k, [128, 1, 128]) for k in range(2)]
                VNS = sbc(es, "VNS", [128, NS]); TS1 = sbc(es, "TS1", [128, NS])
                D1 = sbc(es, "D1", [128, 128]); TM2 = sbc(es, "TM2", [128, 128])
                bk1 = [0]; bkA = [0]; bkB = [0]

                def pb1():
                    bk1[0] += 1
                    return banks[bk1[0] % 2]

                def pbA():
                    bkA[0] += 1
                    return banks[2 + bkA[0] % 3]

                def pbB():
                    bkB[0] += 1
                    return banks[5 + bkB[0] % 2]

                def pbS():
                    return banks[7]

                def rstd_of(dst, src, scale, bias):
                    s.act(dst, src, AF.Ln, scale=scale, bias=bias)
                    s.act(dst, dst, AF.Exp, scale=-0.5)

                def l2norm_a(buf, n, sq):
                    s.act(sq[:, 0:n], buf, AF.Square)

                def l2norm_b(buf, n, qscale, sq, r1, pbf):
                    bank = pbf()
                    s.mm(bank[:, 0:n], ones_b[:], sq[:, 0:n])
                    if qscale:
                        rstd_of(r1[:, 0:n], bank[:, 0:n], 128.0, 128.0 * EPS)
                    else:
                        rstd_of(r1[:, 0:n], bank[:, 0:n], 1.0, EPS)
                    s.tt(buf, buf, r1[:, 0:n], ALU.mult)

                def l2norm(buf, n, qscale, sq, r1, pbf):
                    s.act(sq[:, 0:n], buf, AF.Square)
                    bank = pbf()
                    s.mm(bank[:, 0:n], ones_b[:], sq[:, 0:n])
                    if qscale:
                        rstd_of(r1[:, 0:n], bank[:, 0:n], 128.0, 128.0 * EPS)
                    else:
                        rstd_of(r1[:, 0:n], bank[:, 0:n], 1.0, EPS)
                    s.tt(buf, buf, r1[:, 0:n], ALU.mult)

                def gated_norm_a(ot, n, sq):
                    s.act(sq[:, 0:n], ot, AF.Square)

                def gated_norm_b(h, ot, zs, g0, n, sq, r1, tmp, pbf):
                    bank = pbf()
                    s.mm(bank[:, 0:n], ones_b[:], sq[:, 0:n])
                    rstd_of(r1[:, 0:n], bank[:, 0:n], 1.0 / 128, EPS)
                    s.tt(tmp[:, 0:n], ot, r1[:, 0:n], ALU.mult)
                    s.stt(OG[:, h, g0:g0 + n], tmp[:, 0:n], dnw[:, 0:1], zs, op0=ALU.mult, op1=ALU.mult)

                def gated_norm(h, ot, zs, g0, n, sq, r1, tmp, pbf):
                    s.act(sq[:, 0:n], ot, AF.Square)
                    bank = pbf()
                    s.mm(bank[:, 0:n], ones_b[:], sq[:, 0:n])
                    rstd_of(r1[:, 0:n], bank[:, 0:n], 1.0 / 128, EPS)
                    s.tt(tmp[:, 0:n], ot, r1[:, 0:n], ALU.mult)
                    s.stt(OG[:, h, g0:g0 + n], tmp[:, 0:n], dnw[:, 0:1], zs, op0=ALU.mult, op1=ALU.mult)

                sfront = [True]
                sgen = [None]

                def bg_step():
                    if sgen[0] is not None:
                        try:
                            next(sgen[0])
                        except StopIteration:
                            sgen[0] = None

                def load_wq(h):
                    cols = [h * 128, 1024 + h * 128, 2048 + h * 128, C_Z + h * 128]
                    for i in range(4):
                        wload(Wq[i][:], w_in[l, :, cols[i]:cols[i] + 128])

                def genA1(u):
                    h, g = divmod(u, 4)
                    QK = QKd[u % 2]
                    g0 = g * 512
                    if g == 0:
                        if h == 0:
                            load_wq(0)
                        for ci in range(3):
                            cq = ci * 8 + h
                            for j in range(4):
                                s.ts(DGq[:, ci, j, :], idb[:], cw[:, cq, j:j + 1], None, op0=ALU.mult)
                            s.memset(PRE[ci][:, 0:3], 0.0)
                        yield
                    bz = pb1()
                    proj(bz, Wq[3], 8, HT, g0, 512)
                    s.act(ZSg[u % 3][:, :], bz[:, :], AF.Silu)
                    for ci in range(3):
                        cq = ci * 8 + h
                        bank = pb1()
                        proj(bank, Wq[ci], 8, HT, g0, 512)
                        if g > 0:
                            s.cp(PRE[ci][:, 0:3], PRE[ci][:, 512:515])
                        s.cp(PRE[ci][:, 3:515], bank[:, :], eng=("act" if ci != 1 else "dve"))
                        if g == 3:
                            s.cp(TAILQ[:, cq, 0:3], bank[:, 509:512])
                        if ci != 1:
                            yield
                    for ci in range(3):
                        bank = pb1()
                        for j in range(4):
                            s.mm(bank[:, :], DGq[:, ci, j, :], PRE[ci][:, j:j + 512], start=(j == 0), stop=(j == 3))
                        s.act(QK[ci][:, :], bank[:, :], AF.Silu)
                        if ci != 0:
                            yield
                    if g == 3 and h + 1 < NH:
                        while not sfront[0]:
                            bg_step()
                        load_wq(h + 1)
                    l2norm_a(QK[0][:, :], 512, SQB)
                    yield
                    l2norm_b(QK[0][:, :], 512, True, SQB, R1, pb1)
                    l2norm_a(QK[1][:, :], 512, SQB)
                    yield
                    l2norm_b(QK[1][:, :], 512, False, SQB, R1, pb1)
                    yield

                def genA2(u):
                    h, g = divmod(u, 4)
                    par = u % 2
                    QK = QKd[par]
                    g0 = g * 512
                    c0 = g * 4
                    bc = lambda t: t[:, c0:c0 + 4, h:h + 1].broadcast_to([128, 4, 128])
                    s.tt(DIAG4[:], idf[:].unsqueeze(1).broadcast_to([128, 4, 128]), bc(GC), ALU.mult)
                    bank = pbA(); pv = bfv(bank)
                    for cc in range(4):
                        cs = slice(cc * 128, (cc + 1) * 128)
                        s.tr(pv[:, cc * 128:(cc + 1) * 128], QK[1][:, cs], idb[:])
                        s.tr(pv[:, 512 + cc * 128:512 + (cc + 1) * 128], QK[2][:, cs], idb[:])
                    bankG = pbA()
                    s.mm(bankG[:, :], ones_f[:], DIAG4[:].rearrange("p c n -> p (c n)"))
                    bKK = pbA()
                    for cc in range(4):
                        cs = slice(cc * 128, (cc + 1) * 128)
                        s.mm(bKK[:, cs], QK[1][:, cs], QK[1][:, cs])
                    pk = pv[:, 0:512].rearrange("p (c n) -> p c n", c=4)
                    s.tt(KGC[par][:], pk, bc(EGC), ALU.mult)
                    s.tt(KG[par][:], pk, bc(KSC2), ALU.mult)
                    s.cp(VT[par][:].rearrange("p c n -> p (c n)"), pv[:, 512:1024], eng="act")
                    for cc in range(4):
                        s.stt(DTm4[:, cc, :], bankG[:, cc * 128:(cc + 1) * 128], GC[:, c0 + cc, h:h + 1], negmask[:],
                              op0=ALU.subtract, op1=ALU.add)
                    s.act(EGB4[:].rearrange("p c n -> p (c n)"), bankG[:, :], AF.Exp)
                    s.act(DECI4[:], DTm4[:], AF.Exp)
                    yield
                    bKQ = pbA()
                    for cc in range(4):
                        cs = slice(cc * 128, (cc + 1) * 128)
                        s.mm(bKQ[:, cs], QK[1][:, cs], QK[0][:, cs])
                    s.tt(QG[par][:].rearrange("p c n -> p (c n)"), QK[0][:, :], EGB4[:].rearrange("p c n -> p (c n)"), ALU.mult)
                    s.tt(DECS4[:], DECI4[:], strict01[:].unsqueeze(1).broadcast_to([128, 4, 128]), ALU.mult, eng="pool")
                    s.tt(DTm4[:], bKK[:, :].rearrange("p (c n) -> p c n", c=4), bc(BETA), ALU.mult)
                    for pr in range(2):
                        s.tt(MLp[pr][:, :, 0, :], DTm4[:, 2 * pr:2 * pr + 2, :], DECS4[:, 2 * pr:2 * pr + 2, :], ALU.mult)
                    s.tt(QKT[par][:], bKQ[:, :].rearrange("p (c n) -> p c n", c=4), DECI4[:], ALU.mult)
                    yield
                    for pr in range(2):
                        bank = pbA()
                        for i2 in range(2):
                            s.tr(bank[:, i2 * 128:(i2 + 1) * 128], MLp[pr][:, i2, 0, :], idf[:])
                        s.cp(MLp[pr][:, :, 1, :], bank[:, 0:256].rearrange("p (c n) -> p c n", c=2), eng=("act" if pr == 0 else "dve"))
                        s.tt(PPp[pr][:], idf[:].unsqueeze(1).broadcast_to([128, 2, 128]), MLp[pr][:, :, 0, :], ALU.subtract)
                        yield
                    for lev in range(1, 7):
                        for pr in range(2):
                            m = MLp[pr]
                            b1 = pbA()
                            for i2 in range(2):
                                if lev < 6:
                                    s.mm(b1[:, i2 * 256:i2 * 256 + 128], m[:, i2, 1, :], m[:, i2, 0, :])
                                s.mm(b1[:, i2 * 256 + 128:i2 * 256 + 256], m[:, i2, 0, :], m[:, i2, 1, :])
                            src = b1[:, :].rearrange("p (c t n) -> p c t n", c=2, t=2)
                            if lev < 6:
                                s.cp(m[:, :, :, :], src, eng=("act" if pr == 0 else "dve"))
                            else:
                                s.cp(m[:, :, 1, :], src[:, :, 1, :], eng=("act" if pr == 0 else "dve"))
                            yield
                        for pr in range(2):
                            m = MLp[pr]
                            b3 = pbA()
                            for i2 in range(2):
                                s.mm(b3[:, i2 * 128:(i2 + 1) * 128], m[:, i2, 1, :], PPp[pr][:, i2, :])
                            pa = PPp[pr][:].rearrange("p c n -> p (c n)")
                            s.tt(pa, b3[:, 0:256], pa, ALU.add)
                            yield
                    for pr in range(2):
                        s.cp(Pb[par][:, 2 * pr:2 * pr + 2, :], PPp[pr][:], eng=("act" if pr == 0 else "dve"))
                    yield
                    bank = pbA()
                    for cc in range(4):
                        s.mm(bank[:, cc * 128:(cc + 1) * 128], KGC[par][:, cc, :], Pb[par][:, cc, :])
                    s.ts(NWT[par][:].rearrange("p c n -> p (c n)"), bank[:, :], -1.0, None, op0=ALU.mult)
                    yield

                def genB(u):
                    h, g = divmod(u, 4)
                    par = u % 2
                    g0 = g * 512
                    if g == 0:
                        s.memset(S[:], 0.0); s.memset(Sb[:], 0.0)
                    for cc in range(4):
                        c = g * 4 + cc
                        cs = slice(cc * 128, (cc + 1) * 128)
                        bA = pbB()
                        s.mm(bA[:, 0:128], Pb[par][:, cc, :], VT[par][:, cc, :], start=True, stop=False)
                        s.mm(bA[:, 0:128], NWT[par][:, cc, :], Sb[:], start=False, stop=True)
                        s.ts(VN[:], bA[:, 0:128], BETA[:, c, h:h + 1], None, op0=ALU.mult)
                        yield
                        bC = pbB()
                        s.mm(bC[:, 0:128], Sb[:], QG[par][:, cc, :], start=True, stop=False)
                        s.mm(bC[:, 0:128], VN[:], QKT[par][:, cc, :], start=False, stop=True)
                        s.cp(OTg[:, cs], bC[:, 0:128], eng="act")
                        bD = pbB()
                        s.mm(bD[:, 0:128], KG[par][:, cc, :], VN[:])
                        s.stt(S[:], S[:], GLAST[:, c, h:h + 1], bD[:, 0:128], op0=ALU.mult, op1=ALU.add)
                        s.cp(Sb[:], S[:], eng="act")
                        yield
                    gated_norm_a(OTg[:, :], 512, SQB2)
                    yield
                    gated_norm_b(h, OTg[:, :], ZSg[u % 3][:, :], g0, 512, SQB2, R1b, TMPn, pbB)
                    if g == 3:
                        s.dma(nd_p[l, h], S[:])
                    yield

                def genS(h):
                    bank = pbS()
                    proj(bank, Wq[3], 8, HT, T, NS)
                    s.act(ZSs[:, :], bank[:, 0:NS], AF.Silu)
                    yield
                    for ci in range(3):
                        cq = ci * 8 + h
                        bank = pbS()
                        proj(bank, Wq[ci], 8, HT, T, NS)
                        s.cp(PREs[:, ci, :], bank[:, 0:NS], eng="act")
                        s.cp(TAILQ[:, cq, 3:19], bank[:, 0:NS])
                        yield
                        bank = pbS()
                        for j in range(3):
                            s.mm(bank[:, 0:NS], DGq[:, ci, j, :], SQT[:, cq, :, j], start=(j == 0), stop=False)
                        s.mm(bank[:, 0:NS], DGq[:, ci, 3, :], PREs[:, ci, :], start=False, stop=True)
                        s.act(QKs[:, ci, :], bank[:, 0:NS], AF.Silu)
                        yield
                    sfront[0] = True
                    l2norm_a(QKs[:, 0, :], NS, SQB3)
                    yield
                    l2norm_b(QKs[:, 0, :], NS, True, SQB3, R1c, pbS)
                    l2norm_a(QKs[:, 1, :], NS, SQB3)
                    yield
                    l2norm_b(QKs[:, 1, :], NS, False, SQB3, R1c, pbS)
                    yield
                    s.dma(SDd[0][:], sdelta[l, 0:1, h].rearrange("s k v -> k s v"))
                    yield
                    for sm in range(NS):
                        SD_ = SDd[sm % 2]; SDn_ = SDnd[sm % 2]
                        bR = pbS()
                        s.mm(bR[:, 0:1], SD_[:, 0, :], QKs[:, 1, sm:sm + 1])
                        if sm + 1 < NS:
                            s.dma(SDd[(sm + 1) % 2][:], sdelta[l, sm + 1:sm + 2, h].rearrange("s k v -> k s v"))
                        s.tt(TS1[:, 0:1], bR[:, 0:1], EGS[:, sm, h:h + 1], ALU.mult)
                        s.tt(TS1[:, 0:1], QKs[:, 2, sm:sm + 1], TS1[:, 0:1], ALU.subtract)
                        s.tt(VNS[:, 0:1], TS1[:, 0:1], BETAS[:, sm, h:h + 1], ALU.mult)
                        s.ts(D1[:], idf[:], VNS[:, 0:1], None, op0=ALU.mult)
                        yield
                        bV = pbS()
                        s.mm(bV[:, 0:128], ones_f[:], D1[:])
                        s.ts(TM2[:], bV[:, 0:128], QKs[:, 1, sm:sm + 1], None, op0=ALU.mult)
                        s.stt(SDn_[:, 0, :], SD_[:, 0, :], EGS[:, sm, h:h + 1], TM2[:], op0=ALU.mult, op1=ALU.add)
                        yield
                        bO = pbS()
                        s.mm(bO[:, 0:1], SDn_[:, 0, :], QKs[:, 0, sm:sm + 1])
                        s.cp(OTs[:, sm:sm + 1], bO[:, 0:1])
                        s.dma(nd_s[l, sm:sm + 1, h].rearrange("s k v -> k s v"), SDn_[:])
                        yield
                    gated_norm_a(OTs[:, 0:NS], NS, SQB3)
                    yield
                    gated_norm_b(h, OTs[:, 0:NS], ZSs[:, :], T, NS, SQB3, R1c, TMPs, pbS)
                    yield

                def run_streams(gens, weights, bg=None):
                    live = [[g, w] for g, w in zip(gens, weights) if g is not None]
                    while live:
                        for item in list(live):
                            for _ in range(item[1]):
                                try:
                                    next(item[0])
                                except StopIteration:
                                    live.remove(item)
                                    break
                        if bg is not None and bg[1]:
                            try:
                                next(bg[0])
                            except StopIteration:
                                bg[1] = False

                def step_all(gens, weights):
                    live = [[g_, w_] for g_, w_ in zip(gens, weights) if g_ is not None]
                    while live:
                        for item in list(live):
                            for _ in range(item[1]):
                                try:
                                    next(item[0])
                                except StopIteration:
                                    live.remove(item)
                                    break
                        bg_step()

                NU = NH * 4
                for st in range(-2, NU):
                    a1 = st + 2
                    gA1 = genA1(a1) if a1 < NU else None
                    if gA1 is not None and a1 % 4 == 0:
                        while sgen[0] is not None:
                            bg_step()
                        next(gA1)
                        sfront[0] = False
                        sgen[0] = genS(a1 // 4)
                    step_all([gA1, genA2(st + 1) if 0 <= st + 1 < NU else None, genB(st) if st >= 0 else None], [1, 3, 1])
                while sgen[0] is not None:
                    bg_step()
                    ckpt(5)
                barrier()
            with ExitStack() as es:
                TQ = sbc(es, "TQ", [19, QKV])
                for c4 in range(6):
                    bank = pb()
                    for cc in range(4):
                        c = c4 * 4 + cc
                        s.tr(bank[0:19, cc * 128:(cc + 1) * 128], TAILQ[:, c, :], idf[:])
                    s.cp(TQ[:, c4 * 512:(c4 + 1) * 512], bank[0:19, :])
                s.dma(nq_p[l], TQ[0:3, :])
                s.dma(nq_s[l, :, 2, :], TQ[3:19, :])
                barrier()

            ckpt(6)
            CB = sbc(esL, "CB", [128, 4, TT], BF16)
            with ExitStack() as es:
                UH = sbc(es, "UH", [128, 4, NS, 30], BF16)
                with ExitStack() as es2:
                    ST2 = sbc(es2, "ST2", [120, 4, 512])
                    s.dma(ST2[:], sconf[l].rearrange("s j c -> (s j) c").rearrange("(r p) c -> p r c", p=120))
                    for r in range(4):
                        bank = pb()
                        for c in range(4):
                            s.tr(bank[:, c * 120:(c + 1) * 120], ST2[0:120, r, c * 128:(c + 1) * 128], idf[0:120, 0:120])
                        s.cp(UH[:, :, r * 4:(r + 1) * 4, :].rearrange("p c s j -> p c (s j)"),
                             bank[:, 0:480].rearrange("p (c x) -> p c x", c=4), eng="act")
                    barrier()
                ccw = sbc(es, "ccw", [128, 4, 31]); ccb = sbc(es, "ccb", [128, 4]); lnw = sbc(es, "lnw", [128, 4]); lnb = sbc(es, "lnb", [128, 4])
                for j in range(31):
                    s.dma(ccw[:, :, j], conf_conv_w[l, j].rearrange("(c p) -> p c", p=128))
                fm_vec(ccb[:], conf_conv_b[l]); fm_vec(lnw[:], conf_ln_w[l]); fm_vec(lnb[:], conf_ln_b[l])
                Wa = sbc(es, "Wa", [128, 8, 128], BF16); Wb = sbc(es, "Wb", [128, 8, 128], BF16)
                U1 = sbc(es, "U1", [128, 30 + TT], BF16)
                DGc = sbc(es, "DGc", [128, 31, 128], BF16)
                SG = sbc(es, "SG", [128, 512]); UF = sbc(es, "UF", [128, 512])
                UT = sbc(es, "UT", [128, 4, 46])
                s.memset(U1[:, 0:30], 0.0)
                for c in range(4):
                    wload(Wa[:], w_in[l, :, C_GLU + c * 128:C_GLU + (c + 1) * 128])
                    wload(Wb[:], w_in[l, :, C_GLU + 512 + c * 128:C_GLU + 512 + (c + 1) * 128])
                    for j in range(31):
                        s.ts(DGc[:, j, :], idb[:], ccw[:, c, j:j + 1], None, op0=ALU.mult)
                    for (g0, gw) in GROUPS:
                        b1 = pb(); b2 = pb()
                        proj(b1, Wa, 8, HT, g0, gw)
                        proj(b2, Wb, 8, HT, g0, gw)
                        s.act(SG[:, 0:gw], b2[:, 0:gw], AF.Sigmoid)
                        s.tt(UF[:, 0:gw], b1[:, 0:gw], SG[:, 0:gw], ALU.mult)
                        s.cp(U1[:, 30 + g0:30 + g0 + gw], UF[:, 0:gw], eng="act")
                        if g0 == 1536:
                            s.cp(UT[:, c, 0:30], UF[:, 482:512])
                        if g0 == T:
                            s.cp(UT[:, c, 30:46], UF[:, 0:NS])
                    for (g0, gw) in GROUPS:
                        bank = pb()
                        for j in range(31):
                            if g0 < T:
                                rhs = U1[:, g0 + j:g0 + j + gw]
                            else:
                                rhs = UH[:, c, :, j] if j < 30 else U1[:, 30 + T:30 + TT]
                            s.mm(bank[:, 0:gw], DGc[:, j, :], rhs, start=(j == 0), stop=(j == 30))
                        s.ts(CB[:, c, g0:g0 + gw], bank[:, 0:gw], ccb[:, c:c + 1], None, op0=ALU.add)
                TC = sbc(es, "TC", [46, 512])
                bank = pb()
                for c in range(4):
                    s.tr(bank[0:46, c * 128:(c + 1) * 128], UT[:, c, :], idf[:])
                s.cp(TC[:, :], bank[0:46, :])
                s.dma(nc_p[l], TC[0:30, :])
                s.dma(nc_s[l, :, 29, :], TC[30:46, :])
                SQc = sbc(es, "SQc", [128, 512], BF16)
                MEAN = SG; MSQ = UF; VAR = sbc(es, "VAR", [128, 512]); RS = sbc(es, "RS", [128, 512])
                TMPc = sbc(es, "TMPc", [128, 512])
                for (g0, gw) in GROUPS:
                    bm = pb(); bq = pb()
                    for c in range(4):
                        s.mm(bm[:, 0:gw], ones_b[:], CB[:, c, g0:g0 + gw], start=(c == 0), stop=(c == 3))
                    for c in range(4):
                        s.act(SQc[:, 0:gw], CB[:, c, g0:g0 + gw], AF.Square)
                        s.mm(bq[:, 0:gw], ones_b[:], SQc[:, 0:gw], start=(c == 0), stop=(c == 3))
                    s.ts(MEAN[:, 0:gw], bm[:, 0:gw], 1.0 / 512, None, op0=ALU.mult)
                    s.tt(MSQ[:, 0:gw], MEAN[:, 0:gw], MEAN[:, 0:gw], ALU.mult)
                    s.stt(VAR[:, 0:gw], bq[:, 0:gw], 1.0 / 512, MSQ[:, 0:gw], op0=ALU.mult, op1=ALU.subtract)
                    s.ts(VAR[:, 0:gw], VAR[:, 0:gw], 0.0, None, op0=ALU.max)
                    s.act(RS[:, 0:gw], VAR[:, 0:gw], AF.Ln, bias=EPS)
                    s.act(RS[:, 0:gw], RS[:, 0:gw], AF.Exp, scale=-0.5)
                    for c in range(4):
                        s.tt(TMPc[:, 0:gw], CB[:, c, g0:g0 + gw], MEAN[:, 0:gw], ALU.subtract)
                        s.tt(TMPc[:, 0:gw], TMPc[:, 0:gw], RS[:, 0:gw], ALU.mult)
                        s.act(CB[:, c, g0:g0 + gw], TMPc[:, 0:gw], AF.Silu, scale=lnw[:, c:c + 1], bias=lnb[:, c:c + 1])
                barrier()

            ckpt(7)
            with ExitStack() as es:
                MG = sbc(es, "MG", [128, 4, TT], BF16)
                W3 = [[sbc(es, "WgA%d" % k, [128, 8, 128], BF16), sbc(es, "WgB%d" % k, [128, 8, 128], BF16),
                       sbc(es, "Wod%d" % k, [128, 8, 128], BF16), sbc(es, "Woc%d" % k, [128, 4, 128], BF16)] for k in range(2)]
                Wout = sbc(es, "Wout", [128, 4, D], BF16)
                SA = sbc(es, "SA", [128, 512]); SBt = sbc(es, "SBt", [128, 512])

                def load_w3(c):
                    w = W3[c % 2]
                    wload(w[0][:], w_in[l, :, C_GA + c * 128:C_GA + (c + 1) * 128])
                    wload(w[1][:], w_in[l, :, C_GB + c * 128:C_GB + (c + 1) * 128])
                    wload(w[2][:], w_o_delta[l, :, c * 128:(c + 1) * 128])
                    wload(w[3][:], w_o_conf[l, :, c * 128:(c + 1) * 128])

                load_w3(0)
                for grp in range(2):
                    wload(Wout[:], w_out[l, grp * 512:(grp + 1) * 512, :])
                    for cc in range(4):
                        c = grp * 4 + cc
                        if c + 1 < 8:
                            load_w3(c + 1)
                        WgA, WgB, Wod, Woc = W3[c % 2]
                        for (g0, gw) in GROUPS:
                            b1 = pb(); b2 = pb(); b3 = pb(); b4 = pb()
                            proj(b1, WgA, 8, HT, g0, gw)
                            proj(b2, WgB, 8, HT, g0, gw)
                            proj(b3, Wod, 8, OG, g0, gw)
                            proj(b4, Woc, 4, CB, g0, gw)
                            s.act(SA[:, 0:gw], b1[:, 0:gw], AF.Sigmoid)
                            s.act(SBt[:, 0:gw], b2[:, 0:gw], AF.Sigmoid)
                            s.tt(SA[:, 0:gw], b3[:, 0:gw], SA[:, 0:gw], ALU.mult)
                            s.tt(SBt[:, 0:gw], b4[:, 0:gw], SBt[:, 0:gw], ALU.mult)
                            s.tt(MG[:, cc, g0:g0 + gw], SA[:, 0:gw], SBt[:, 0:gw], ALU.add, eng="pool")
                    for ti in range(17):
                        t0, n = TILES[ti]
                        xa, _ = Xt(ti)
                        for hf in range(2):
                            bank = pb()
                            for cc in range(4):
                                s.mm(bank[0:n, :], MG[:, cc, t0:t0 + n], Wout[:, cc, hf * 512:(hf + 1) * 512], start=(cc == 0), stop=(cc == 3))
                            s.tt(xa[0:n, hf * 512:(hf + 1) * 512], xa[0:n, hf * 512:(hf + 1) * 512], bank[0:n, :], ALU.add)
                barrier()
            ckpt(8)
            esL.close()

            norm_to_HT(norm_ffn_w[l])
            with ExitStack() as es:
                FH = sbc(es, "FH", [128, 44, NS, 2], BF16)
                FT = sbc(es, "FT", [128, 44, 18])
                fcw = sbc(es, "fcw", [128, 44, 3]); fcb = sbc(es, "fcb", [128, 44])
                for j in range(3):
                    s.dma(fcw[:, :, j], ffn_conv_w[l, j].rearrange("(c p) -> p c", p=128))
                fm_vec(fcb[:], ffn_conv_b[l])
                with ExitStack() as es2:
                    ST3 = sbc(es2, "ST3", [32, 2 * DFF])
                    s.dma(ST3[:, :], sffn[l].rearrange("s j c -> (s j) c"))
                    for c16 in range(3):
                        bank = pb()
                        ncs = min(16, 44 - c16 * 16)
                        for cc in range(ncs):
                            c = c16 * 16 + cc
                            s.tr(bank[:, cc * 32:(cc + 1) * 32], ST3[0:32, c * 128:(c + 1) * 128], idf[0:32, 0:32])
                        s.cp(FH[:, c16 * 16:c16 * 16 + ncs, :, :].rearrange("p c s j -> p (c s j)"), bank[:, 0:ncs * 32], eng="act")
                    barrier()
                with ExitStack() as es2:
                    A = sbc(es2, "A", [128, 11, TT], BF16)
                    UPd = [[sbc(es2, "UP%d_%d" % (k, i), [128, 2 + TT], BF16) for i in range(2)] for k in range(2)]
                    Wud = [sbc(es2, "Wu%d" % k, [128, 8, 2, 128], BF16) for k in range(2)]
                    DGfd = [sbc(es2, "DGf%d" % k, [128, 2, 3, 128], BF16) for k in range(2)]
                    Wdh = [sbc(es2, "Wd%d" % k, [128, 11, 512], BF16) for k in range(2)]
                    SGt = [sbc(es2, "SGt", [128, 512])] * 2
                    for k in range(2):
                        for i in range(2):
                            s.memset(UPd[k][i][:, 0:2], 0.0)
                    bkP = [0]; bkC = [0]

                    def pbP():
                        bkP[0] += 1
                        return banks[bkP[0] % 4]

                    def pbC():
                        bkC[0] += 1
                        return banks[4 + bkC[0] % 4]

                    def load_wu(c):
                        for i, ch in enumerate((c, 22 + c)):
                            wload(Wud[c % 2][:, :, i, :], w_up[l, :, ch * 128:(ch + 1) * 128])

                    def genP(c):
                        par = c % 2
                        chs = (c, 22 + c)
                        UP = UPd[par]; DGf = DGfd[par]
                        Wu = Wud[par]
                        if c + 1 < 22:
                            load_wu(c + 1)
                        for i in range(2):
                            for j in range(3):
                                s.ts(DGf[:, i, j, :], idb[:], fcw[:, chs[i], j:j + 1], None, op0=ALU.mult)
                        yield
                        for (g0, gw) in GROUPS:
                            for i in range(2):
                                bank = pbP()
                                proj(bank, Wu[:, :, i, :], 8, HT, g0, gw)
                                s.cp(UP[i][:, 2 + g0:2 + g0 + gw], bank[:, 0:gw], eng=("act" if i == 0 else "dve"))
                                if g0 == 1536:
                                    s.cp(FT[:, chs[i], 0:2], bank[:, 510:512])
                                if g0 == T:
                                    s.cp(FT[:, chs[i], 2:18], bank[:, 0:NS])
                                yield

                    def genC(c):
                        par = c % 2
                        cc = c % 11
                        chs = (c, 22 + c)
                        UP = UPd[par]; DGf = DGfd[par]
                        for gi, (g0, gw) in enumerate(GROUPS):
                            bb = [pbC(), pbC()]
                            for i in range(2):
                                for j in range(3):
                                    if g0 < T:
                                        rhs = UP[i][:, g0 + j:g0 + j + gw]
                                    else:
                                        rhs = FH[:, chs[i], :, j] if j < 2 else UP[i][:, 2 + T:2 + TT]
                                    s.mm(bb[i][:, 0:gw], DGf[:, i, j, :], rhs, start=(j == 0), stop=(j == 2))
                            sg = SGt[gi % 2]
                            s.act(sg[:, 0:gw], bb[0][:, 0:gw], AF.Silu, bias=fcb[:, chs[0]:chs[0] + 1])
                            s.stt(A[:, cc, g0:g0 + gw], bb[1][:, 0:gw], fcb[:, chs[1]:chs[1] + 1], sg[:, 0:gw], op0=ALU.add, op1=ALU.mult)
                            yield

                    def genD(grp):
                        for hf in range(2):
                            Wd = Wdh[hf]
                            for ti in range(17):
                                t0, n = TILES[ti]
                                xa, _ = Xt(ti)
                                bank = pbC()
                                for cc in range(11):
                                    s.mm(bank[0:n, :], A[:, cc, t0:t0 + n], Wd[:, cc, :], start=(cc == 0), stop=(cc == 10))
                                s.tt(xa[0:n, hf * 512:(hf + 1) * 512], xa[0:n, hf * 512:(hf + 1) * 512], bank[0:n, :], ALU.add)
                                yield

                    def step2(gens, weights):
                        live = [[g_, w_] for g_, w_ in zip(gens, weights) if g_ is not None]
                        while live:
                            for item in list(live):
                                for _ in range(item[1]):
                                    try:
                                        next(item[0])
                                    except StopIteration:
                                        live.remove(item)
                                        break

                    load_wu(0)
                    step2([genP(0)], [1])
                    for c in range(22):
                        if c % 11 == 0:
                            for hf in range(2):
                                wload(Wdh[hf][:], w_down[l, (c // 11) * 1408:(c // 11 + 1) * 1408, hf * 512:(hf + 1) * 512])
                        step2([genP(c + 1) if c + 1 < 22 else None, genC(c)], [2, 1])
                        if c % 11 == 10:
                            step2([genD(c // 11)], [1])
                    barrier()
                with ExitStack() as es2:
                    TF = sbc(es2, "TF", [18, 2 * DFF])
                    for c4 in range(11):
                        bank = pb()
                        for cc in range(4):
                            c = c4 * 4 + cc
                            s.tr(bank[0:18, cc * 128:(cc + 1) * 128], FT[:, c, :], idf[:])
                        s.cp(TF[:, c4 * 512:(c4 + 1) * 512], bank[0:18, :])
                    s.dma(nf_p[l], TF[0:2, :])
                    s.dma(nf_s[l, :, 1, :], TF[2:18, :])
                    barrier()

        except _Stop:
            pass
        with ExitStack() as es:
            junk = sbc(es, "junkf", [128, D], BF16)
            wbc = sbc(es, "wbc", [128, D])
            Y = [sbc(es, "Y%d" % i, [128, D]) for i in range(2)]
            s.dma(wbc[:], norm_final_w.rearrange("(o d) -> o d", o=1).broadcast_to([128, D]))
            for ti in range(17):
                xa, n = Xt(ti)
                s.act(junk[0:n, :], xa[0:n], AF.Square, accum_out=ss[0:n, ti:ti + 1])
            s.ts(rstd[:], ss[:], 1.0 / D, EPS, op0=ALU.mult, op1=ALU.add)
            s.act(rstd[:], rstd[:], AF.Sqrt)
            s.recip(rstd[:], rstd[:])
            for ti in range(17):
                xa, n = Xt(ti)
                t0 = TILES[ti][0]
                y = Y[ti % 2]
                s.stt(y[0:n, :], xa[0:n], rstd[0:n, ti:ti + 1], wbc[0:n, :], op0=ALU.mult, op1=ALU.mult)
                if ti < 16:
                    s.dma(yp[t0:t0 + 128, :], y[:, :])
                else:
                    s.dma(ys, y[0:NS, :])
            barrier()
        s.emit()
    return nc, s


_CACHE = {}

IN_NAMES = ["norm_mix_w", "w_in", "conv_qkv_w", "a_log", "dt_bias", "delta_norm_w", "w_o_delta", "conf_conv_w",
            "conf_conv_b", "conf_ln_w", "conf_ln_b", "w_o_conf", "w_out", "norm_ffn_w", "w_up", "ffn_conv_w",
            "ffn_conv_b", "w_down", "norm_final_w"]


def kernel(x_prompt, x_sample, state_delta, state_qkv_conv, state_conf_conv, state_ffn_conv, _cores=None, _upto=99, **w):
    cores = list(range(8)) if _cores is None else list(_cores)
    if _upto not in _CACHE:
        _CACHE[_upto] = build_program(upto=_upto)[0]
    nc = _CACHE[_upto]
    f = lambda a: np.ascontiguousarray(np.asarray(a, dtype=np.float32))
    wts = {k: f(w[k]) for k in IN_NAMES}
    in_maps = []
    for c in cores:
        m = dict(wts)
        m["xp"] = f(x_prompt[c]); m["xs"] = f(x_sample[16 * c:16 * c + 16, 0])
        m["sdelta"] = f(state_delta[:, 16 * c:16 * c + 16]); m["sqkv"] = f(state_qkv_conv[:, 16 * c:16 * c + 16])
        m["sconf"] = f(state_conf_conv[:, 16 * c:16 * c + 16]); m["sffn"] = f(state_ffn_conv[:, 16 * c:16 * c + 16])
        in_maps.append(m)
    res = run_bass_kernel_spmd(nc, in_maps, core_ids=list(range(len(cores))))
    R = res.results
    nco = len(cores)
    y_prompt = np.stack([R[i]["yp"] for i in range(nco)], 0)
    y_sample = np.concatenate([R[i]["ys"] for i in range(nco)], 0)[:, None, :]
    pst = lambda k: np.stack([R[i][k] for i in range(nco)], 1)
    sst = lambda k: np.concatenate([R[i][k] for i in range(nco)], 1)
    return (y_prompt.astype(np.float32), y_sample.astype(np.float32), pst("nd_p"), pst("nq_p"), pst("nc_p"), pst("nf_p"),
            sst("nd_s"), sst("nq_s"), sst("nc_s"), sst("nf_s"))
```
